# Optimizing a Trainium2 kernel written in Bass

```python
import math
import jax
import jax.numpy as jnp
from jax import lax
import numpy as np

D_MODEL = 1024
BATCH = 16
SEQ = 2048
DEPTH = 4

CTX_LEN = 256
GRID_W = 64
CONV_W = 4
EPS = 1e-6

LRU_WIDTH = D_MODEL
LRU_BLOCKS = 8
LRU_BW = LRU_WIDTH // LRU_BLOCKS
LRU_C = 8.0

GLA_HEADS = 4
GLA_DK = D_MODEL // 2
GLA_DV = D_MODEL
GLA_DKH = GLA_DK // GLA_HEADS
GLA_DVH = GLA_DV // GLA_HEADS
GLA_RANK = 16
GLA_TAU = 16.0
GLA_CHUNK = 64

SSD_INNER = 2 * D_MODEL
SSD_HEADDIM = 64
SSD_HEADS = SSD_INNER // SSD_HEADDIM
SSD_STATE = 128
SSD_GROUPS = 4
SSD_HPG = SSD_HEADS // SSD_GROUPS
SSD_XBC = SSD_INNER + 2 * SSD_GROUPS * SSD_STATE
SSD_CHUNK = 64

N_BRANCH = 3
IN_WIDTHS = (LRU_WIDTH, LRU_WIDTH, GLA_DK, GLA_DK, GLA_DV, GLA_DV, 2 * GLA_RANK,
             SSD_INNER, SSD_XBC, 2 * SSD_HEADS, N_BRANCH * D_MODEL)
IN_TOTAL = sum(IN_WIDTHS)

kernel_name = 'hybrid_lru_gla_ssd_prefix_dit'


def _split_cols(u):
    parts = []
    start = 0
    for w in IN_WIDTHS:
        parts.append(u[..., start:start + w])
        start += w
    return parts


def _rmsnorm(x, g):
    xf = x.astype(jnp.float32)
    y = xf * lax.rsqrt(jnp.mean(xf * xf, axis=-1, keepdims=True) + EPS)
    return (y * g.astype(jnp.float32)).astype(x.dtype)


def _to_col_major(h, rows):
    b, length, dm = h.shape
    return h.reshape(b, rows, GRID_W, dm).transpose(0, 2, 1, 3).reshape(b, length, dm)


def _from_col_major(h, rows):
    b, length, dm = h.shape
    return h.reshape(b, GRID_W, rows, dm).transpose(0, 2, 1, 3).reshape(b, length, dm)


def _dwconv(u, w, b, line_len):
    bn, length, ch = u.shape
    ul = u.reshape(bn, length // line_len, line_len, ch)
    left = (CONV_W - 1) // 2
    up = jnp.pad(ul, ((0, 0), (0, 0), (left, CONV_W - 1 - left), (0, 0)))
    out = b + w[0] * up[:, :, 0:line_len]
    for k in range(1, CONV_W):
        out = out + w[k] * up[:, :, k:k + line_len]
    return out.reshape(bn, length, ch)


def _flip(ts):
    return tuple(jnp.flip(t, axis=1) for t in ts)


def _bidir(step, ctx_f, lat_f, ctx_b, lat_b, s0, need_ctx):
    yc_f, sc_f = step(*ctx_f, s0)
    yl_f, _ = step(*lat_f, sc_f)
    yc_b, sc_b = step(*_flip(ctx_b), s0)
    yl_b, _ = step(*_flip(lat_b), sc_b)
    y_lat = yl_f + jnp.flip(yl_b, axis=1)
    y_ctx = yc_f + jnp.flip(yc_b, axis=1) if need_ctx else None
    return y_ctx, y_lat


def _to_chunks(t, chunk):
    b, length = t.shape[:2]
    return jnp.moveaxis(t.reshape((b, length // chunk, chunk) + t.shape[2:]), 1, 0)


def _from_chunks(t):
    n, b, chunk = t.shape[:3]
    return jnp.moveaxis(t, 0, 1).reshape((b, n * chunk) + t.shape[3:])


def _lru_gates(xa, wr, br, wi, bi, lam):
    bn, length, width = xa.shape
    xb = xa.reshape(bn, length, LRU_BLOCKS, LRU_BW)
    r = jax.nn.sigmoid(jnp.einsum('blnk,nkj->blnj', xb, wr).reshape(bn, length, width) + br)
    i = jax.nn.sigmoid(jnp.einsum('blnk,nkj->blnj', xb, wi).reshape(bn, length, width) + bi)
    log_a = (-LRU_C * jax.nn.softplus(-lam) * r).astype(jnp.float32)
    a = jnp.exp(log_a)
    b = jnp.sqrt(-jnp.expm1(2.0 * log_a)) * (i * xa).astype(jnp.float32)
    return a, b


def _lru_scan(a, b, h0):
    def combine(lhs, rhs):
        return lhs[0] * rhs[0], rhs[0] * lhs[1] + rhs[1]
    a_cum, h = lax.associative_scan(combine, (a, b), axis=1)
    h = h + a_cum * h0[:, None]
    return h, h[:, -1]


def _gla_chunked(q, k, v, logg, s0):
    mask = jnp.tril(jnp.ones((GLA_CHUNK, GLA_CHUNK), dtype=bool))

    def body(state, xs):
        qc, kc, vc, gc = xs
        bc = jnp.cumsum(gc, axis=1)
        btot = bc[:, -1]
        qe = qc * jnp.exp(bc)
        ke = kc * jnp.exp(-bc)
        att = jnp.where(mask, jnp.einsum('bthk,bshk->bhts', qe, ke), 0.0)
        o = jnp.einsum('bhts,bshv->bthv', att, vc) + jnp.einsum('bthk,bhkv->bthv', qe, state)
        kd = kc * jnp.exp(btot[:, None] - bc)
        state = state * jnp.exp(btot)[..., None] + jnp.einsum('bshk,bshv->bhkv', kd, vc)
        return state, o

    xs = tuple(_to_chunks(t, GLA_CHUNK) for t in (q, k, v, logg))
    state, o = lax.scan(body, s0, xs)
    return _from_chunks(o), state


def _gla_inputs(s, p):
    bn, length = s[2].shape[:2]
    q = (s[2] * GLA_DKH ** -0.5).reshape(bn, length, GLA_HEADS, GLA_DKH)
    k = s[3].reshape(bn, length, GLA_HEADS, GLA_DKH)
    v = s[4].reshape(bn, length, GLA_HEADS, GLA_DVH)
    low = s[6].reshape(bn, length, 2, GLA_RANK)
    dirs = []
    for d in range(2):
        z = jnp.einsum('blr,rk->blk', low[:, :, d], p['gla_alpha_up'][d]) + p['gla_alpha_b'][d]
        logg = jax.nn.log_sigmoid(z.astype(jnp.float32)) / GLA_TAU
        dirs.append((q, k, v, logg.reshape(bn, length, GLA_HEADS, GLA_DKH)))
    return dirs[0], dirs[1]


def _ssd_chunked(xdt, adt, bm, cm, s0):
    idx = jnp.arange(SSD_CHUNK)
    mask = (idx[:, None] >= idx[None, :])[None, :, :, None, None]

    def body(state, xs):
        xc, ac, bc, cc = xs
        cum = jnp.cumsum(ac, axis=1)
        seg = cum[:, :, None] - cum[:, None, :]
        lmat = jnp.exp(jnp.where(mask, seg, -jnp.inf))
        cb = jnp.einsum('btgn,bsgn->btsg', cc, bc)
        y = jnp.einsum('btsgh,bsghp->btghp', cb[..., None] * lmat, xc)
        y = y + jnp.einsum('btgn,bghnp->btghp', cc, state) * jnp.exp(cum)[..., None]
        dec = jnp.exp(cum[:, -1:] - cum)
        state = state * jnp.exp(cum[:, -1])[..., None, None] + jnp.einsum('bsgn,bsghp->bghnp', bc, xc * dec[..., None])
        return state, y

    xs = tuple(_to_chunks(t, SSD_CHUNK) for t in (xdt, adt, bm, cm))
    state, y = lax.scan(body, s0, xs)
    return _from_chunks(y), state


def _ssd_inputs(s, line_len, p):
    bn, length = s[8].shape[:2]
    xbc = jax.nn.silu(_dwconv(s[8], p['conv_c_w'], p['conv_c_b'], line_len))
    gn = SSD_GROUPS * SSD_STATE
    xs = xbc[..., :SSD_INNER].reshape(bn, length, SSD_GROUPS, SSD_HPG, SSD_HEADDIM)
    bm = xbc[..., SSD_INNER:SSD_INNER + gn].reshape(bn, length, SSD_GROUPS, SSD_STATE)
    cm = xbc[..., SSD_INNER + gn:].reshape(bn, length, SSD_GROUPS, SSD_STATE)
    dt_raw = s[9].reshape(bn, length, 2, SSD_HEADS).astype(jnp.float32)
    dirs = []
    for d in range(2):
        dt = jax.nn.softplus(dt_raw[:, :, d] + p['ssd_dt_bias'][d]).reshape(bn, length, SSD_GROUPS, SSD_HPG)
        a = -jnp.exp(p['ssd_a_log'][d].astype(jnp.float32)).reshape(SSD_GROUPS, SSD_HPG)
        dirs.append((xs * dt[..., None], dt * a, bm, cm))
    return xs, dirs[0], dirs[1]


def _finish(s, ya, yb, yc, xs, p):
    bn, length = s[1].shape[:2]
    pa = (ya * jax.nn.silu(s[1])) @ p['w_pa']
    ob = _rmsnorm(yb, p['gla_norm_g']).reshape(bn, length, GLA_DV) * jax.nn.silu(s[5])
    pb = ob @ p['w_pb']
    yc = (yc + p['ssd_d'].reshape(SSD_GROUPS, SSD_HPG, 1) * xs).reshape(bn, length, SSD_INNER)
    gsz = SSD_INNER // SSD_GROUPS
    oc = _rmsnorm((yc * jax.nn.silu(s[7])).reshape(bn, length, SSD_GROUPS, gsz),
                  p['ssd_norm_g'].reshape(SSD_GROUPS, gsz))
    pc = oc.reshape(bn, length, SSD_INNER) @ p['w_pc']
    g = jax.nn.sigmoid(s[10]).reshape(bn, length, N_BRANCH, D_MODEL)
    merged = g[:, :, 0] * pa + g[:, :, 1] * pb + g[:, :, 2] * pc
    return merged @ p['w_out']


def _mixer(h_ctx, h_lat, line_len, need_ctx, p):
    f32 = jnp.float32
    s_c = _split_cols(h_ctx @ p['w_in'])
    s_l = _split_cols(h_lat @ p['w_in'])
    bn = h_lat.shape[0]
    lc = h_ctx.shape[1]

    def lru_in(s, line):
        xa = _dwconv(s[0], p['conv_a_w'], p['conv_a_b'], line)
        return [_lru_gates(xa, p['lru_wr'][d], p['lru_br'][d], p['lru_wi'][d], p['lru_bi'][d], p['lru_lam'][d])
                for d in range(2)]
    a_c = lru_in(s_c, lc)
    a_l = lru_in(s_l, line_len)
    ya_c, ya_l = _bidir(_lru_scan, a_c[0], a_l[0], a_c[1], a_l[1],
                        jnp.zeros((bn, LRU_WIDTH), f32), need_ctx)

    bf_c, bb_c = _gla_inputs(s_c, p)
    bf_l, bb_l = _gla_inputs(s_l, p)
    yb_c, yb_l = _bidir(_gla_chunked, bf_c, bf_l, bb_c, bb_l,
                        jnp.zeros((bn, GLA_HEADS, GLA_DKH, GLA_DVH), f32), need_ctx)

    xs_c, cf_c, cb_c = _ssd_inputs(s_c, lc, p)
    xs_l, cf_l, cb_l = _ssd_inputs(s_l, line_len, p)
    yc_c, yc_l = _bidir(_ssd_chunked, cf_c, cf_l, cb_c, cb_l,
                        jnp.zeros((bn, SSD_GROUPS, SSD_HPG, SSD_STATE, SSD_HEADDIM), f32), need_ctx)

    out_lat = _finish(s_l, ya_l, yb_l, yc_l, xs_l, p)
    out_ctx = _finish(s_c, ya_c, yb_c, yc_c, xs_c, p) if need_ctx else None
    return out_ctx, out_lat


def setup_inputs(seed: int = 0) -> dict:
    key = jax.random.key(seed)
    ks = jax.random.split(key, 32)
    f32 = jnp.float32

    def nrm(k, shape, scale):
        return jax.random.normal(k, shape, f32) * scale

    L = DEPTH
    a0 = jax.random.uniform(ks[20], (L, 2, LRU_WIDTH), f32, 0.9, 0.999)
    p_a = a0 ** (1.0 / LRU_C)
    lru_lam = jnp.log(p_a) - jnp.log1p(-p_a)
    a_init = jax.random.uniform(ks[21], (L, 2, SSD_HEADS), f32, 1.0, 16.0)
    dt0 = jnp.exp(jax.random.uniform(ks[22], (L, 2, SSD_HEADS), f32, math.log(1e-3), math.log(1e-1)))
    return {
        'x': nrm(ks[0], (BATCH, SEQ, D_MODEL), 1.0),
        'c': nrm(ks[1], (BATCH, D_MODEL), 1.0),
        'ctx': nrm(ks[2], (BATCH, CTX_LEN, D_MODEL), 1.0),
        'c_ctx': nrm(ks[3], (D_MODEL,), 1.0),
        'ada_w': nrm(ks[4], (L, D_MODEL, 3 * D_MODEL), 0.5 * D_MODEL ** -0.5),
        'ada_b': nrm(ks[5], (L, 3 * D_MODEL), 0.01),
        'pre_g': 1.0 + nrm(ks[6], (L, D_MODEL), 0.05),
        'post_g': 1.0 + nrm(ks[7], (L, D_MODEL), 0.05),
        'w_in': nrm(ks[8], (L, D_MODEL, IN_TOTAL), D_MODEL ** -0.5),
        'conv_a_w': nrm(ks[9], (L, CONV_W, LRU_WIDTH), CONV_W ** -0.5),
        'conv_a_b': nrm(ks[10], (L, LRU_WIDTH), 0.01),
        'lru_wr': nrm(ks[11], (L, 2, LRU_BLOCKS, LRU_BW, LRU_BW), LRU_BW ** -0.5),
        'lru_br': nrm(ks[12], (L, 2, LRU_WIDTH), 0.01),
        'lru_wi': nrm(ks[13], (L, 2, LRU_BLOCKS, LRU_BW, LRU_BW), LRU_BW ** -0.5),
        'lru_bi': nrm(ks[14], (L, 2, LRU_WIDTH), 0.01),
        'lru_lam': lru_lam,
        'gla_alpha_up': nrm(ks[15], (L, 2, GLA_RANK, GLA_DK), GLA_RANK ** -0.5),
        'gla_alpha_b': nrm(ks[16], (L, 2, GLA_DK), 0.1),
        'gla_norm_g': 1.0 + nrm(ks[17], (L, GLA_DVH), 0.05),
        'conv_c_w': nrm(ks[18], (L, CONV_W, SSD_XBC), CONV_W ** -0.5),
        'conv_c_b': nrm(ks[19], (L, SSD_XBC), 0.01),
        'ssd_a_log': jnp.log(a_init),
        'ssd_dt_bias': dt0 + jnp.log(-jnp.expm1(-dt0)),
        'ssd_d': 1.0 + nrm(ks[23], (L, SSD_HEADS), 0.05),
        'ssd_norm_g': 1.0 + nrm(ks[24], (L, SSD_INNER), 0.05),
        'w_pa': nrm(ks[25], (L, LRU_WIDTH, D_MODEL), LRU_WIDTH ** -0.5),
        'w_pb': nrm(ks[26], (L, GLA_DV, D_MODEL), GLA_DV ** -0.5),
        'w_pc': nrm(ks[27], (L, SSD_INNER, D_MODEL), SSD_INNER ** -0.5),
        'w_out': nrm(ks[28], (L, D_MODEL, D_MODEL), D_MODEL ** -0.5),
    }


def reference(x, c, ctx, c_ctx, ada_w, ada_b, pre_g, post_g, w_in, conv_a_w, conv_a_b,
              lru_wr, lru_br, lru_wi, lru_bi, lru_lam, gla_alpha_up, gla_alpha_b, gla_norm_g,
              conv_c_w, conv_c_b, ssd_a_log, ssd_dt_bias, ssd_d, ssd_norm_g,
              w_pa, w_pb, w_pc, w_out):
    rows = x.shape[1] // GRID_W
    x_lat, x_ctx = x, ctx
    for l in range(DEPTH):
        p = {
            'w_in': w_in[l], 'conv_a_w': conv_a_w[l], 'conv_a_b': conv_a_b[l],
            'lru_wr': lru_wr[l], 'lru_br': lru_br[l], 'lru_wi': lru_wi[l], 'lru_bi': lru_bi[l],
            'lru_lam': lru_lam[l], 'gla_alpha_up': gla_alpha_up[l], 'gla_alpha_b': gla_alpha_b[l],
            'gla_norm_g': gla_norm_g[l], 'conv_c_w': conv_c_w[l], 'conv_c_b': conv_c_b[l],
            'ssd_a_log': ssd_a_log[l], 'ssd_dt_bias': ssd_dt_bias[l], 'ssd_d': ssd_d[l],
            'ssd_norm_g': ssd_norm_g[l], 'w_pa': w_pa[l], 'w_pb': w_pb[l], 'w_pc': w_pc[l],
            'w_out': w_out[l],
        }
        col_major = (l % 2 == 1)
        need_ctx = l < DEPTH - 1
        line_len = rows if col_major else GRID_W

        shift, scale, gate = jnp.split(jax.nn.silu(c) @ ada_w[l] + ada_b[l], 3, axis=-1)
        shift_c, scale_c, gate_c = jnp.split(jax.nn.silu(c_ctx) @ ada_w[l] + ada_b[l], 3, axis=-1)

        h_lat = _rmsnorm(x_lat, pre_g[l]) * (1.0 + scale[:, None]) + shift[:, None]
        h_ctx = _rmsnorm(x_ctx, pre_g[l]) * (1.0 + scale_c) + shift_c
        if col_major:
            h_lat = _to_col_major(h_lat, rows)

        o_ctx, o_lat = _mixer(h_ctx, h_lat, line_len, need_ctx, p)

        if col_major:
            o_lat = _from_col_major(o_lat, rows)
        x_lat = x_lat + gate[:, None] * _rmsnorm(o_lat, post_g[l])
        if need_ctx:
            x_ctx = x_ctx + gate_c * _rmsnorm(o_ctx, post_g[l])
    return x_lat
```

```python
from contextlib import ExitStack
import numpy as np
import concourse.bass as bass
import concourse.mybir as mybir
from concourse.bass_utils import run_bass_kernel_spmd

F32 = mybir.dt.float32
BF16 = mybir.dt.bfloat16
AF = mybir.ActivationFunctionType
ALU = mybir.AluOpType

L = 4
D = 1024
S = 2048
CT = 256
T = CT + S
NCH = T // 128
EPS = 1e-6
NTILES = [(0, 256), (256, 512), (768, 512), (1280, 512), (1792, 512)]
O_LX, O_LG, O_Q, O_K, O_V, O_GG, O_LOW, O_Z, O_XBC, O_DT, O_MG = (
    0, 1024, 2048, 2560, 3072, 4096, 5120, 5152, 7200, 10272, 10336)
BLKS = []
BIDX = {}


def _mkblks():
    def add(name, c0, n, m=128):
        BIDX[name] = len(BLKS)
        for i in range(n):
            BLKS.append((c0 + i * m, m))
    add("lx", O_LX, 8); add("lg", O_LG, 8); add("q", O_Q, 4); add("k", O_K, 4)
    add("v", O_V, 8); add("gg", O_GG, 8); add("low", O_LOW, 2, 16); add("z", O_Z, 16)
    add("xbc", O_XBC, 24); add("dt", O_DT, 1, 64); add("mg", O_MG, 24)


_mkblks()
NBLK = len(BLKS)

PPL = {}
_o = 0
for _n, _w in (("caw", 32), ("cab", 8), ("lbr", 16), ("lbi", 16), ("llam", 16), ("gab", 8), ("gng", 2),
               ("ccw", 96), ("ccb", 24), ("sdd", 16), ("sng", 16), ("sal", 1), ("sdb", 1)):
    PPL[_n] = _o
    _o += _w
PPW = _o
NPP = PPW * L


class Tk:
    def __init__(self, name, h, semkey=None):
        self.name = name
        self.h = h
        self.last_w = None
        self.readers = []
        self.semkey = semkey or ("D_" + name)

    def __getitem__(self, idx):
        return self.h[idx]


class Prog:
    ENG = ("pe", "act", "dve", "pool", "sp")

    def __init__(self, nc):
        self.nc = nc
        self.es = ExitStack()
        self.ops = {e: [] for e in self.ENG}
        self.cnt = {e: 0 for e in self.ENG}
        self.waited = {e: {} for e in self.ENG}
        self.sems = {}
        self.dcount = {}
        for e in self.ENG:
            self._sem("E_" + e)

    def _sem(self, key):
        if key not in self.sems:
            self.sems[key] = self.es.enter_context(self.nc.semaphore("s%d" % len(self.sems)))
        return self.sems[key]

    def sbuf(self, name, shape, dtype):
        return Tk(name, self.es.enter_context(self.nc.sbuf_tensor("sb_" + name, list(shape), dtype)))

    def psum(self, name, shape, dtype=F32):
        return Tk(name, self.es.enter_context(self.nc.psum_tensor("ps_" + name, list(shape), dtype)))

    def dram(self, name, shape, dtype, kind="Internal"):
        return Tk(name, self.nc.dram_tensor(name, list(shape), dtype, kind=kind))

    def _waits(self, eng, reads, writes):
        evs = []
        for t in reads:
            if t.last_w is not None:
                evs.append(t.last_w)
        for t in writes:
            if t.last_w is not None:
                evs.append(t.last_w)
            evs.extend(t.readers)
        w = {}
        wd = self.waited[eng]
        for (k, v) in evs:
            if wd.get(k, 0) >= v:
                continue
            if w.get(k, 0) < v:
                w[k] = v
        for k, v in w.items():
            wd[k] = v
        return list(w.items())

    def _commit(self, ev, reads, writes):
        for t in reads:
            t.readers.append(ev)
        for t in writes:
            t.last_w = ev
            t.readers = []

    def op(self, eng, fn, reads=(), writes=()):
        waits = self._waits(eng, reads, writes)
        self.cnt[eng] += 1
        ev = ("E_" + eng, self.cnt[eng])
        self.ops[eng].append((waits, fn, ev[0], 1))
        self._commit(ev, reads, writes)

    def dma(self, q, out_t, out_ap, in_t, in_ap):
        key = out_t.semkey
        self._sem(key)
        waits = self._waits(q, [in_t], [out_t])
        self.dcount[key] = self.dcount.get(key, 0) + 16
        ev = (key, self.dcount[key])

        def fn(e, out_ap=out_ap, in_ap=in_ap):
            return e.dma_start(out=out_ap, in_=in_ap)
        self.ops[q].append((waits, fn, key, 16))
        self._commit(ev, [in_t], [out_t])

    def barrier(self, tiles):
        for eng in self.ENG:
            waits = self._waits(eng, [], tiles)
            if waits:
                self.ops[eng].append((waits, None, None, 0))

    def emit(self):
        nc = self.nc
        sems = self.sems
        ops = self.ops

        def run(e, lst):
            for (waits, fn, sk, inc) in lst:
                for (k, v) in waits:
                    e.wait_ge(sems[k], v)
                if fn is not None:
                    fn(e).then_inc(sems[sk], inc)

        with nc.Block() as block:
            @block.tensor
            def _(e):
                run(e, ops["pe"])

            @block.scalar
            def _(e):
                run(e, ops["act"])

            @block.vector
            def _(e):
                run(e, ops["dve"])

            @block.gpsimd
            def _(e):
                run(e, ops["pool"])

            @block.sync
            def _(e):
                run(e, ops["sp"])
        self.es.close()


class Ring:
    def __init__(self, tiles):
        self.t = tiles
        self.i = 0

    def next(self):
        t = self.t[self.i % len(self.t)]
        self.i += 1
        return t


class Arena:
    def __init__(self, P, nbytes):
        self.P = P
        self.t = P.sbuf("arena", [128, nbytes // 4], F32)
        self.nbytes = nbytes
        self.off = 0
        self.live = []
        self.gen = 0

    def reset(self):
        self.P.barrier(self.live)
        self.live = []
        self.off = 0
        self.gen += 1

    def view(self, name, shape, dtype, semkey=None):
        esz = 4 if dtype == F32 else 2
        n = 1
        for s in shape[1:]:
            n *= s
        nb = (n * esz + 31) // 32 * 32
        assert self.off + nb <= self.nbytes, (name, self.off, nb, self.nbytes)
        a = self.t.h[0:shape[0], self.off // 4:(self.off + nb) // 4]
        if dtype != F32:
            a = a.bitcast(dtype)
        a = a[:, 0:n]
        if len(shape) == 3:
            a = a.rearrange("p (a b) -> p a b", b=shape[2])
        elif len(shape) == 4:
            a = a.rearrange("p (a b c) -> p a b c", b=shape[2], c=shape[3])
        self.off += nb
        t = Tk("%s_g%d" % (name, self.gen), a, semkey=semkey or ("DA_" + name))
        self.live.append(t)
        return t


class StopBuild(Exception):
    pass


class K:
    def ck(self, tag):
        import os
        if os.environ.get("CK") == tag:
            raise StopBuild(tag)

    def __init__(self, nl=L):
        self.nl = nl
        nc = bass.Bass("TRN2", target_bir_lowering=False)
        self.nc = nc
        self.P = P = Prog(nc)
        d = P.dram
        self.x_d = d("x", [2, S, D], F32, "ExternalInput")
        self.ctx_d = d("ctx", [2, CT, D], F32, "ExternalInput")
        self.cT_d = d("cT", [128, 8, 3], F32, "ExternalInput")
        self.adaw_d = d("ada_w", [L, 128, 8, 3072], F32, "ExternalInput")
        self.rows_d = d("rows", [L, 3, 3072], F32, "ExternalInput")
        self.win_d = d("w_in", [L, NBLK, 128, 8, 128], F32, "ExternalInput")
        self.pp_d = d("pp", [128, NPP], F32, "ExternalInput")
        self.lruw_d = d("lruw", [L, 8, 128, 4, 128], F32, "ExternalInput")
        self.gup_d = d("gup", [L, 16, 2, 512], F32, "ExternalInput")
        self.wpa_d = d("wpa", [L, 8, 128, 8, 128], F32, "ExternalInput")
        self.wpb_d = d("wpb", [L, 8, 128, 8, 128], F32, "ExternalInput")
        self.wpc_d = d("wpc", [L, 8, 128, 16, 128], F32, "ExternalInput")
        self.wout_d = d("wout", [L, 128, 8, 1024], F32, "ExternalInput")
        self.out_d = d("out", [2, S, D], F32, "ExternalOutput")
        self.ctxs_d = d("ctxs", [2, CT, D], F32)
        self.mods_d = d("mods", [L, 3, 3, 1024], F32)

        self.hT = P.sbuf("hT", [128, 8, T], BF16)
        self.merged = P.sbuf("merged", [128, 8, T], BF16)
        self.pp = P.sbuf("pp", [128, NPP], F32)
        self.pd = P.sbuf("pd", [128, L * 32], F32)
        self.identf = P.sbuf("identf", [128, 128], F32)
        self.ident = P.sbuf("ident", [128, 128], BF16)
        self.mf = P.sbuf("mf", [128, 4, 128], F32)
        self.mk = P.sbuf("mk", [128, 4, 128], BF16)
        self.ones = P.sbuf("ones", [128, 128], BF16)
        self.onef = P.sbuf("onef", [128, 1], F32)
        self.stat = Ring([P.sbuf("stat%d" % i, [128, 4], F32) for i in range(4)])
        self.wring = Ring([P.sbuf("wblk%d" % i, [128, 8, 128], BF16) for i in range(4)])
        self.pring = Ring([P.sbuf("pblk%d" % i, [128, 8, 128], BF16) for i in range(2)])
        self.lring = Ring([P.sbuf("lblk%d" % i, [128, 4, 128], BF16) for i in range(2)])
        self.gup = P.sbuf("gup", [16, 2, 512], BF16)
        self.bring = Ring([P.sbuf("btmp%d" % i, [128, 512], BF16) for i in range(3)])
        used = (2 * 8 * T * 2 + NPP * 4 + L * 32 * 4 + 512 + 256 + 2048 + 1024 + 256 + 4 + 4 * 16
                + 4 * 2048 + 2 * 2048 + 2 * 1024 + 2048 + 3 * 1024)
        self.A = Arena(P, (211900 - used) // 32 * 32)
        self.pmm = Ring([P.psum("pmm%d" % i, [128, 512]) for i in range(3)])
        self.pA = P.psum("pA", [128, 1024])
        self.pB = P.psum("pB", [128, 1024])
        self.pT = P.psum("pT", [128, 1024], BF16)

    def act(self, ot, oap, it, iap, func, bias=None, scale=None, accum=None, rd=(), wr=()):
        kw = {}
        if bias is not None:
            kw["bias"] = bias
        if scale is not None:
            kw["scale"] = scale
        if accum is not None:
            kw["accum_out"] = accum
        self.P.op("act", lambda e: e.activation(oap, iap, func, **kw), [it] + list(rd), [ot] + list(wr))

    def tt(self, ot, oap, at, aap, bt, bap, op, eng="dve"):
        self.P.op(eng, lambda e: e.tensor_tensor(oap, aap, bap, op), [at, bt], [ot])

    def ts(self, ot, oap, at, aap, s1, s2, op0, op1=None, rd=(), eng="dve"):
        if op1 is None:
            self.P.op(eng, lambda e: e.tensor_scalar(oap, aap, s1, None, op0), [at] + list(rd), [ot])
        else:
            self.P.op(eng, lambda e: e.tensor_scalar(oap, aap, s1, s2, op0, op1), [at] + list(rd), [ot])

    def stt(self, ot, oap, at, aap, sc, bt, bap, op0, op1, rd=(), eng="dve"):
        self.P.op(eng, lambda e: e.scalar_tensor_tensor(oap, aap, sc, bap, op0, op1), [at, bt] + list(rd), [ot])

    def cp(self, ot, oap, it, iap, eng="dve"):
        if eng == "act":
            self.P.op("act", lambda e: e.activation(oap, iap, AF.Copy), [it], [ot])
        else:
            self.P.op(eng, lambda e: e.tensor_copy(oap, iap), [it], [ot])

    def mm(self, ot, groups, reads):
        def fn(e):
            ins = None
            for (oap, pairs) in groups:
                n = len(pairs)
                for i, (l_, r_) in enumerate(pairs):
                    ins = e.matmul(oap, l_, r_, start=(i == 0), stop=(i == n - 1))
            return ins
        self.P.op("pe", fn, reads, [ot])

    def tr(self, ot, items, reads, ident):
        def fn(e):
            ins = None
            for (oap, iap) in items:
                ins = e.transpose(oap, iap, ident)
            return ins
        self.P.op("pe", fn, reads, [ot])

    def rstd(self, st, col, n_inv):
        a = st[:, col:col + 1]
        self.ts(st, a, st, a, n_inv, EPS, ALU.mult, ALU.add)
        self.act(st, a, st, a, AF.Sqrt)
        self.P.op("dve", lambda e: e.reciprocal(a, a), [st], [st])

    def ppc(self, l, name, i=0):
        c = l * PPW + PPL[name] + i
        return self.pp[:, c:c + 1]

    def win(self, l, bi, consume, tiles=NTILES):
        c0, M = BLKS[bi]
        wt = self.wring.next()
        self.P.dma("pool", wt, wt[:, :, :], self.win_d, self.win_d[l, bi])
        for (n0, nsz) in tiles:
            ps = self.pmm.next()
            self.mm(ps, [(ps[0:M, 0:nsz], [(wt[:, k, 0:M], self.hT[:, k, n0:n0 + nsz]) for k in range(8)])],
                    [wt, self.hT])
            consume(ps, n0, nsz)

    def conv(self, l, ps, n0, nsz, dst_t, dst_ap, wname, bname, blk, ll):
        if n0 == 0:
            ll = nsz
        w = [self.ppc(l, wname, blk * 4 + k) for k in range(4)]
        self.act(dst_t, dst_ap, ps, ps[:, 0:nsz], AF.Identity, bias=self.ppc(l, bname, blk), scale=w[1], rd=[self.pp])
        d3 = dst_ap.rearrange("p (a b) -> p a b", b=ll)
        p3 = ps[:, 0:nsz].rearrange("p (a b) -> p a b", b=ll)
        for (k, so, do_) in ((0, slice(0, ll - 1), slice(1, ll)), (2, slice(1, ll), slice(0, ll - 1)),
                             (3, slice(2, ll), slice(0, ll - 2))):
            self.stt(dst_t, d3[:, :, do_], ps, p3[:, :, so], w[k], dst_t, d3[:, :, do_], ALU.mult, ALU.add,
                     rd=[self.pp])

    def setup(self):
        P = self.P
        P.dma("sp", self.pp, self.pp[:, :], self.pp_d, self.pp_d[:, :])
        P.op("pool", lambda e: e.memset(self.identf[:, :], 0.0), [], [self.identf])
        P.op("pool", lambda e: e.affine_select(self.identf[:, :], self.identf[:, :], [[-1, 128]], ALU.not_equal,
                                               1.0, base=0, channel_multiplier=1), [self.identf], [self.identf])
        self.cp(self.ident, self.ident[:, :], self.identf, self.identf[:, :])
        P.op("pool", lambda e: e.memset(self.mf[:, :, :], 1.0), [], [self.mf])
        for i, (pat, cm, cmp_) in enumerate((([[-1, 128]], 1, ALU.is_gt), ([[1, 128]], -1, ALU.is_gt),
                                             ([[1, 128]], -1, ALU.is_ge), ([[-1, 128]], 1, ALU.is_ge))):
            P.op("pool", lambda e, i=i, pat=pat, cm=cm, cmp_=cmp_: e.affine_select(
                self.mf[:, i, :], self.mf[:, i, :], pat, cmp_, 0.0, base=0, channel_multiplier=cm),
                [self.mf], [self.mf])
        self.cp(self.mk, self.mk[:, :, :], self.mf, self.mf[:, :, :])
        P.op("dve", lambda e: e.memset(self.ones[:, :], 1.0), [], [self.ones])
        P.op("dve", lambda e: e.memset(self.onef[:, :], 1.0), [], [self.onef])
        for l in range(L):
            b = l * 32
            lam = self.pp[:, l * PPW + PPL["llam"]: l * PPW + PPL["llam"] + 16]
            o = self.pd[:, b:b + 16]
            self.act(self.pd, o, self.pp, lam, AF.Exp, scale=-1.0)
            self.act(self.pd, o, self.pd, o, AF.Ln, bias=1.0)
            self.ts(self.pd, o, self.pd, o, -8.0, None, ALU.mult)
            gb = self.pp[:, l * PPW + PPL["gab"]: l * PPW + PPL["gab"] + 8]
            self.ts(self.pd, self.pd[:, b + 16:b + 24], self.pp, gb, -1.0, None, ALU.mult)
            al = self.ppc(l, "sal")
            self.act(self.pd, self.pd[:, b + 24:b + 25], self.pp, al, AF.Exp)
            self.ts(self.pd, self.pd[:, b + 24:b + 25], self.pd, self.pd[:, b + 24:b + 25], -1.0, None, ALU.mult)
        self.TL, self.TU, self.UF, self.UB = (self.mk[:, i, :] for i in range(4))

    def adaln(self):
        P, A = self.P, self.A
        A.reset()
        scT = A.view("scT", [128, 8, 3], F32)
        P.dma("sp", scT, scT[:, :, :], self.cT_d, self.cT_d[:, :, :])
        self.act(scT, scT[:, :, :], scT, scT[:, :, :], AF.Silu)
        awr = Ring([A.view("adaw%d" % i, [128, 8, 512], F32) for i in range(2)])
        rows = [A.view("rows%d" % i, [3, 3072], F32) for i in range(3)]
        modr = A.view("modr", [3, 3072], F32)
        modA = A.view("modA", [3, 2048], F32)
        for l in range(self.nl):
            for i in range(3):
                P.dma("sp", rows[i], rows[i][:, :], self.rows_d, self.rows_d[l, i:i + 1, :].partition_broadcast(3))
            for n in range(6):
                aw = awr.next()
                P.dma("sp", aw, aw[:, :, :], self.adaw_d, self.adaw_d[l, :, :, n * 512:(n + 1) * 512])
                ps = self.pmm.next()
                self.mm(ps, [(ps[0:3, :], [(scT[:, k, :], aw[:, k, :]) for k in range(8)])], [scT, aw])
                self.tt(modr, modr[:, n * 512:(n + 1) * 512], ps, ps[0:3, :], rows[0], rows[0][:, n * 512:(n + 1) * 512],
                        ALU.add)
            self.stt(modA, modA[:, 0:1024], modr, modr[:, 1024:2048], 1.0, rows[1], rows[1][:, 0:1024], ALU.add, ALU.mult)
            self.tt(modA, modA[:, 1024:2048], modr, modr[:, 2048:3072], rows[2], rows[2][:, 0:1024], ALU.mult)
            P.dma("sp", self.mods_d, self.mods_d[l, :, 0, :], modA, modA[:, 0:1024])
            P.dma("sp", self.mods_d, self.mods_d[l, :, 1, :], modr, modr[:, 0:1024])
            P.dma("sp", self.mods_d, self.mods_d[l, :, 2, :], modA, modA[:, 1024:2048])

    def xdma(self, l, b, j, sb_t, sb_ap, load, first_layer_src):
        P = self.P
        if j < 2:
            dt_ = (self.ctx_d if (load and l == 0) else self.ctxs_d)
            aps = [(dt_[b, j * 128:(j + 1) * 128, :], sb_ap)]
        else:
            dt_ = (self.x_d if (load and l == 0) else self.out_d)
            jj = j - 2
            if l % 2 == 0:
                aps = [(dt_[b, jj * 128:(jj + 1) * 128, :], sb_ap)]
            else:
                v = dt_[b].rearrange("(r w) d -> w r d", w=64)
                aps = [(v[jj * 4 + wl], sb_ap[wl * 32:(wl + 1) * 32, :]) for wl in range(4)]
        for (da, sa) in aps:
            if load:
                P.dma("sp", sb_t, sa, dt_, da)
            else:
                P.dma("sp", dt_, da, sb_t, sa)

    def bload(self, t, l, row, which):
        self.P.dma("sp", t, t[:, :], self.mods_d, self.mods_d[l, row, which:which + 1, :].partition_broadcast(128))

    def phaseA(self, l, b):
        P, A = self.P, self.A
        A.reset()
        Ab = A.view("Ab", [128, D], F32); Sb = A.view("Sb", [128, D], F32)
        Ac = A.view("Ac", [128, D], F32); Sc = A.view("Sc", [128, D], F32)
        self.bload(Ab, l, b, 0); self.bload(Sb, l, b, 1); self.bload(Ac, l, 2, 0); self.bload(Sc, l, 2, 1)
        xr = Ring([A.view("xa%d" % i, [128, D], F32) for i in range(3)])
        t1r = Ring([A.view("t1_%d" % i, [128, D], F32) for i in range(2)])
        hbr = Ring([A.view("hb%d" % i, [128, D], BF16) for i in range(2)])
        junk = A.view("junk", [128, D], BF16)
        for j in range(NCH):
            xt = xr.next(); t1 = t1r.next(); hb = hbr.next(); st = self.stat.next()
            self.xdma(l, b, j, xt, xt[:, :], True, None)
            P.op("dve", lambda e, st=st: e.memset(st[:, 0:1], 0.0), [], [st])
            self.act(junk, junk[:, :], xt, xt[:, :], AF.Square, accum=st[:, 0:1], wr=[st])
            self.rstd(st, 0, 1.0 / D)
            Aa, Sa = (Ac, Sc) if j < 2 else (Ab, Sb)
            self.stt(t1, t1[:, :], xt, xt[:, :], st[:, 0:1], Aa, Aa[:, :], ALU.mult, ALU.mult, rd=[st])
            self.tt(hb, hb[:, :], t1, t1[:, :], Sa, Sa[:, :], ALU.add)
            self.tr(self.pT, [(self.pT[:, k * 128:(k + 1) * 128], hb[:, k * 128:(k + 1) * 128]) for k in range(8)],
                    [hb, self.ident], self.ident[:, :])
            self.cp(self.hT, self.hT[:, :, j * 128:(j + 1) * 128],
                    self.pT, self.pT[:, :].rearrange("p (a b) -> p a b", b=128), eng=("act" if j % 2 else "dve"))

    def project(self, l, bo_t, bo_ap, nk, w_d, gi, mode, tiles):
        P = self.P
        for m in range(8):
            wts = []
            for k0 in range(0, nk, 8):
                wt = self.pring.next()
                P.dma("pool", wt, wt[:, :, :], w_d, w_d[l, m, :, k0:k0 + 8, :])
                wts.append(wt)
            gt = self.wring.next()
            bi = BIDX["mg"] + gi * 8 + m
            P.dma("pool", gt, gt[:, :, :], self.win_d, self.win_d[l, bi])
            for (n0, nsz) in tiles:
                pg = self.pmm.next()
                self.mm(pg, [(pg[:, 0:nsz], [(gt[:, k, :], self.hT[:, k, n0:n0 + nsz]) for k in range(8)])], [gt, self.hT])
                g = self.bring.next()
                self.act(g, g[:, 0:nsz], pg, pg[:, 0:nsz], AF.Sigmoid)
                pp_ = self.pmm.next()
                self.mm(pp_, [(pp_[:, 0:nsz], [(wts[k // 8][:, k % 8, :], bo_ap[:, k, n0:n0 + nsz]) for k in range(nk)])],
                        wts + [bo_t])
                ma = self.merged[:, m, n0:n0 + nsz]
                if mode == "set":
                    self.tt(self.merged, ma, pp_, pp_[:, 0:nsz], g, g[:, 0:nsz], ALU.mult)
                else:
                    t = self.bring.next()
                    self.tt(t, t[:, 0:nsz], pp_, pp_[:, 0:nsz], g, g[:, 0:nsz], ALU.mult)
                    self.tt(self.merged, ma, self.merged, ma, t, t[:, 0:nsz], ALU.add)

    def lru(self, l, b, tiles, mode):
        P, A = self.P, self.A
        A.reset()
        ll = 32 if l % 2 else 64
        ya = A.view("ya", [128, 8, T], BF16)
        xa = A.view("xa", [128, T], F32); xab = A.view("xab", [128, T], BF16)
        rt = A.view("rt", [128, T], F32); it = A.view("it", [128, T], F32); tmp = A.view("ltmp", [128, T], F32)
        hf = A.view("hf", [128, T], F32); hbk = A.view("hbk", [128, T], F32)
        for n in range(8):
            lw = self.lring.next()
            P.dma("pool", lw, lw[:, :, :], self.lruw_d, self.lruw_d[l, n])
            self.win(l, BIDX["lx"] + n, lambda ps, n0, nsz: self.conv(l, ps, n0, nsz, xa, xa[:, n0:n0 + nsz], "caw", "cab", n, ll))
            self.cp(xab, xab[:, :], xa, xa[:, :], eng="act")
            for d in range(2):
                for (n0, nsz) in NTILES:
                    for (ri, dst, bn) in ((0, rt, "lbr"), (1, it, "lbi")):
                        ps = self.pmm.next()
                        self.mm(ps, [(ps[:, 0:nsz], [(lw[:, ri * 2 + d, :], xab[:, n0:n0 + nsz])])], [lw, xab])
                        self.act(dst, dst[:, n0:n0 + nsz], ps, ps[:, 0:nsz], AF.Sigmoid, bias=self.ppc(l, bn, d * 8 + n),
                                 rd=[self.pp])
                cl = self.pd[:, l * 32 + d * 8 + n: l * 32 + d * 8 + n + 1]
                self.act(rt, rt[:, :], rt, rt[:, :], AF.Exp, scale=cl, rd=[self.pd])
                self.act(tmp, tmp[:, :], rt, rt[:, :], AF.Square)
                self.ts(tmp, tmp[:, :], tmp, tmp[:, :], -1.0, 1.0, ALU.mult, ALU.add)
                self.ts(tmp, tmp[:, :], tmp, tmp[:, :], 0.0, None, ALU.max)
                self.act(tmp, tmp[:, :], tmp, tmp[:, :], AF.Sqrt)
                self.tt(it, it[:, :], it, it[:, :], xa, xa[:, :], ALU.mult)
                self.tt(it, it[:, :], it, it[:, :], tmp, tmp[:, :], ALU.mult)
                if d == 0:
                    P.op("dve", lambda e: e.tensor_tensor_scan(hf[:, :], rt[:, :], it[:, :], 0.0, ALU.mult, ALU.add),
                         [rt, it], [hf])
                else:
                    P.op("dve", lambda e: e.tensor_tensor_scan(hbk[:, CT - 1::-1], rt[:, CT - 1::-1], it[:, CT - 1::-1],
                                                               0.0, ALU.mult, ALU.add), [rt, it], [hbk])
                    P.op("dve", lambda e: e.tensor_tensor_scan(hbk[:, T - 1:CT - 1:-1], rt[:, T - 1:CT - 1:-1],
                                                               it[:, T - 1:CT - 1:-1], hbk[:, 0:1], ALU.mult, ALU.add),
                         [rt, it, hbk], [hbk])
            self.tt(hf, hf[:, :], hf, hf[:, :], hbk, hbk[:, :], ALU.add)

            def cons(ps, n0, nsz, n=n):
                g = self.bring.next()
                self.act(g, g[:, 0:nsz], ps, ps[:, 0:nsz], AF.Silu)
                self.tt(ya, ya[:, n, n0:n0 + nsz], hf, hf[:, n0:n0 + nsz], g, g[:, 0:nsz], ALU.mult)
            self.win(l, BIDX["lg"] + n, cons)
        self.project(l, ya, ya, 8, self.wpa_d, 0, mode, tiles)

    def gla(self, l, b, tiles, mode):
        P, A = self.P, self.A
        A.reset()
        ob = A.view("ob", [128, 8, T], BF16)
        low = [A.view("low%d" % d, [16, T], BF16) for d in range(2)]
        vtok = A.view("vtok", [128, NCH, 256], BF16)
        qT = A.view("qT", [128, T], BF16); kT = A.view("kT", [128, T], BF16)
        qe = A.view("qe", [128, T], BF16); ke = A.view("ke", [128, T], BF16); kd = A.view("kd", [128, T], BF16)
        tA = A.view("tA", [128, T], F32); tB = A.view("tB", [128, T], F32); G = A.view("G", [128, T + 1], F32)
        vTa = tB.h.bitcast(BF16)[:, 0:T]
        ebt = A.view("ebt", [128, NCH], F32)
        Sf = A.view("Sf", [128, 256], F32); Sb = A.view("Sbf", [128, 256], BF16)
        kdt = Ring([A.view("kdt%d" % i, [128, 128], BF16) for i in range(2)])
        atm = Ring([A.view("atm%d" % i, [128, 128], BF16) for i in range(2)])
        P.dma("pool", self.gup, self.gup[:, :, :], self.gup_d, self.gup_d[l])
        for d in range(2):
            self.win(l, BIDX["low"] + d, lambda ps, n0, nsz, d=d: self.cp(low[d], low[d][:, n0:n0 + nsz], ps, ps[0:16, 0:nsz], eng="act"))
        for h in range(4):
            self.win(l, BIDX["q"] + h, lambda ps, n0, nsz: self.act(qT, qT[:, n0:n0 + nsz], ps, ps[:, 0:nsz], AF.Copy, scale=128.0 ** -0.5))
            self.win(l, BIDX["k"] + h, lambda ps, n0, nsz: self.cp(kT, kT[:, n0:n0 + nsz], ps, ps[:, 0:nsz], eng="act"))
            for j in range(2):
                self.win(l, BIDX["v"] + h * 2 + j, lambda ps, n0, nsz: self.cp(tB, vTa[:, n0:n0 + nsz], ps, ps[:, 0:nsz], eng="act"))
                for c0 in range(0, NCH, 8):
                    cn = min(8, NCH - c0)
                    self.tr(self.pT, [(self.pT[:, i * 128:(i + 1) * 128], vTa[:, (c0 + i) * 128:(c0 + i + 1) * 128]) for i in range(cn)],
                            [tB, self.ident], self.ident[:, :])
                    self.cp(vtok, vtok[:, c0:c0 + cn, j * 128:(j + 1) * 128],
                            self.pT, self.pT[:, 0:cn * 128].rearrange("p (a b) -> p a b", b=128))
            obh = ob[:, 2 * h:2 * h + 2, :]
            for d in range(2):
                nb = self.pd[:, l * 32 + 16 + d * 4 + h: l * 32 + 16 + d * 4 + h + 1]
                for (n0, nsz) in NTILES:
                    ps = self.pmm.next()
                    self.mm(ps, [(ps[:, 0:nsz], [(self.gup[:, d, h * 128:(h + 1) * 128], low[d][:, n0:n0 + nsz])])], [self.gup, low[d]])
                    self.act(tA, tA[:, n0:n0 + nsz], ps, ps[:, 0:nsz], AF.Exp, bias=nb, scale=-1.0, rd=[self.pd])
                self.act(tA, tA[:, :], tA, tA[:, :], AF.Ln, bias=1.0)
                onesb = self.onef[:, 0:1].to_broadcast([128, T])
                if d == 0:
                    P.op("dve", lambda e: e.memset(G[:, 0:1], 0.0), [], [G])
                    P.op("dve", lambda e: e.tensor_tensor_scan(G[:, 1:T + 1], onesb, tA[:, :], 0.0, ALU.mult, ALU.add), [tA, self.onef], [G])
                    hi = G[:, 1:T + 1].rearrange("p (c j) -> p c j", j=128)
                    lo = G[:, 0:T:128].unsqueeze(2).to_broadcast([128, NCH, 128])
                    tot = tA[:, 127:T:128]
                else:
                    P.op("dve", lambda e: e.memset(G[:, T:T + 1], 0.0), [], [G])
                    P.op("dve", lambda e: e.tensor_tensor_scan(G[:, T - 1::-1], onesb, tA[:, ::-1], 0.0, ALU.mult, ALU.add), [tA, self.onef], [G])
                    hi = G[:, 0:T].rearrange("p (c j) -> p c j", j=128)
                    lo = G[:, 128:T + 1:128].unsqueeze(2).to_broadcast([128, NCH, 128])
                    tot = tA[:, 0:T:128]
                tA3 = tA[:, :].rearrange("p (c j) -> p c j", j=128)
                tB3 = tB[:, :].rearrange("p (c j) -> p c j", j=128)
                self.tt(tA, tA3, G, hi, G, lo, ALU.subtract)
                self.act(tB, tB[:, :], tA, tA[:, :], AF.Exp, scale=-1.0 / 16)
                self.tt(qe, qe[:, :], qT, qT[:, :], tB, tB[:, :], ALU.mult)
                self.act(tB, tB[:, :], tA, tA[:, :], AF.Exp, scale=1.0 / 16)
                self.tt(ke, ke[:, :], kT, kT[:, :], tB, tB[:, :], ALU.mult)
                self.tt(tB, tB3, tA, tA3, tA, tot.unsqueeze(2).to_broadcast([128, NCH, 128]), ALU.subtract)
                self.act(tB, tB[:, :], tB, tB[:, :], AF.Exp, scale=1.0 / 16)
                self.tt(kd, kd[:, :], kT, kT[:, :], tB, tB[:, :], ALU.mult)
                self.act(ebt, ebt[:, :], tA, tot, AF.Exp, scale=-1.0 / 16)
                P.op("dve", lambda e: e.memset(Sf[:, :], 0.0), [], [Sf])
                P.op("dve", lambda e: e.memset(Sb[:, :], 0.0), [], [Sb])
                order = list(range(NCH)) if d == 0 else [1, 0] + list(range(NCH - 1, 1, -1))
                msk = self.UF if d == 0 else self.UB
                for c in order:
                    cs = slice(c * 128, (c + 1) * 128)
                    kt = kdt.next(); am = atm.next()
                    self.tr(self.pT, [(self.pT[:, 0:128], kd[:, cs])], [kd, self.ident], self.ident[:, :])
                    self.cp(kt, kt[:, :], self.pT, self.pT[:, 0:128], eng="act")
                    pa = self.pmm.next()
                    self.mm(pa, [(pa[:, 0:128], [(ke[:, cs], qe[:, cs])])], [ke, qe])
                    self.tt(am, am[:, :], pa, pa[:, 0:128], self.mk, msk, ALU.mult)
                    po = self.pmm.next()
                    self.mm(po, [(po[:, j * 128:(j + 1) * 128], [(vtok[:, c, j * 128:(j + 1) * 128], am[:, :]),
                                                                  (Sb[:, j * 128:(j + 1) * 128], qe[:, cs])]) for j in range(2)],
                            [vtok, am, Sb, qe])
                    po3 = po[:, 0:256].rearrange("p (a b) -> p a b", b=128)
                    if d == 0:
                        self.cp(ob, obh[:, :, cs], po, po3, eng="act")
                    else:
                        self.tt(ob, obh[:, :, cs], po, po3, ob, obh[:, :, cs], ALU.add)
                    pS = self.pmm.next()
                    self.mm(pS, [(pS[:, 0:256], [(kt[:, :], vtok[:, c, :])])], [kt, vtok])
                    self.stt(Sf, Sf[:, :], Sf, Sf[:, :], ebt[:, c:c + 1], pS, pS[:, 0:256], ALU.mult, ALU.add, rd=[ebt])
                    self.cp(Sb, Sb[:, :], Sf, Sf[:, :], eng="act")
            for (n0, nsz) in NTILES:
                pss = self.pmm.next()
                sqs = []
                for j in range(2):
                    sq = self.bring.next()
                    self.tt(sq, sq[:, 0:nsz], ob, obh[:, j, n0:n0 + nsz], ob, obh[:, j, n0:n0 + nsz], ALU.mult)
                    sqs.append(sq)
                self.mm(pss, [(pss[:, 0:nsz], [(self.ones[:, :], sq[:, 0:nsz]) for sq in sqs])], sqs + [self.ones])
                self.ts(tA, tA[:, n0:n0 + nsz], pss, pss[:, 0:nsz], 1.0 / 256, EPS, ALU.mult, ALU.add)
            self.act(tA, tA[:, :], tA, tA[:, :], AF.Sqrt)
            P.op("dve", lambda e: e.reciprocal(tA[:, :], tA[:, :]), [tA], [tA])
            for j in range(2):
                def cons(ps, n0, nsz, j=j):
                    g = self.bring.next()
                    self.act(g, g[:, 0:nsz], ps, ps[:, 0:nsz], AF.Silu)
                    oa = obh[:, j, n0:n0 + nsz]
                    self.stt(ob, oa, ob, oa, self.ppc(l, "gng", j), tA, tA[:, n0:n0 + nsz], ALU.mult, ALU.mult, rd=[self.pp])
                    self.tt(ob, oa, ob, oa, g, g[:, 0:nsz], ALU.mult)
                self.win(l, BIDX["gg"] + h * 2 + j, cons)
        self.project(l, ob, ob, 8, self.wpb_d, 1, mode, tiles)

    def ssd(self, l, b, tiles):
        P, A = self.P, self.A
        A.reset()
        ll = 32 if l % 2 else 64
        bo = A.view("bo", [128, 4, T], BF16)
        xtok = A.view("xtok", [128, NCH, 512], BF16)
        BT = A.view("BT", [128, T], BF16); CTt = A.view("CT", [128, T], BF16); Btok = A.view("Btok", [128, NCH, 128], BF16)
        dtk = A.view("dtk", [128, NCH, 64], F32); adk = A.view("adk", [128, NCH, 64], F32)
        adb = A.view("adb", [128, NCH, 64], BF16); ddt = A.view("ddt", [128, NCH, 64], F32)
        dtot = A.view("dtot", [128, NCH, 64], F32)
        xsf = Ring([A.view("xsf%d" % i, [128, T], BF16) for i in range(2)])
        ctmp = Ring([A.view("ctmp%d" % i, [128, 512], F32) for i in range(2)])
        Rr = Ring([A.view("R%d" % i, [128, 8, 128], BF16) for i in range(2)])
        LTr = Ring([A.view("LT%d" % i, [128, 8, 128], BF16) for i in range(2)])
        ECr = Ring([A.view("EC%d" % i, [128, 8, 128], BF16) for i in range(2)])
        cbr = Ring([A.view("cbm%d" % i, [128, 128], BF16) for i in range(2)])
        xdr = Ring([A.view("xd%d" % i, [128, 8, 64], BF16) for i in range(4)])
        Sf = A.view("Sf", [128, 8, 64], F32); Sb = A.view("Sbf", [128, 512], BF16)
        rs = A.view("rs", [128, 512], F32)
        sdb = A.view("sdbc", [128, 128], F32)
        P.dma("sp", sdb, sdb[:, :], self.rows_d, self.rows_d[l, 1:2, 1024:1152].partition_broadcast(128))
        self.act(sdb, sdb[:, 64:128], sdb, sdb[:, 64:128], AF.Exp)
        self.ts(sdb, sdb[:, 64:128], sdb, sdb[:, 64:128], -1.0, None, ALU.mult)
        wdt = self.wring.next()
        P.dma("pool", wdt, wdt[:, :, :], self.win_d, self.win_d[l, BIDX["dt"]])
        for c0 in range(0, NCH, 8):
            cn = min(8, NCH - c0)
            ps = self.pmm.next()
            self.mm(ps, [(ps[:, i * 64:(i + 1) * 64], [(self.hT[:, k, (c0 + i) * 128:(c0 + i + 1) * 128], wdt[:, k, 0:64]) for k in range(8)])
                         for i in range(cn)], [self.hT, wdt])
            da = dtk[:, c0:c0 + cn, :]
            self.tt(dtk, da, ps, ps[:, 0:cn * 64].rearrange("p (a b) -> p a b", b=64), sdb,
                    sdb[:, 0:64].unsqueeze(1).to_broadcast([128, cn, 64]), ALU.add)
            self.act(dtk, da, dtk, da, AF.Exp)
            self.act(dtk, da, dtk, da, AF.Ln, bias=1.0)
        self.ck("dt0")
        self.tt(adk, adk[:, :, :], dtk, dtk[:, :, :], sdb, sdb[:, 64:128].unsqueeze(1).to_broadcast([128, NCH, 64]), ALU.mult)
        self.ck("dt1")
        self.cp(adb, adb[:, :, :], adk, adk[:, :, :])
        for c0 in range(0, NCH, 8):
            cn = min(8, NCH - c0)
            ps = self.pmm.next()
            self.mm(ps, [(ps[:, i * 64 + d * 32: i * 64 + d * 32 + 32], [((self.TL if d == 0 else self.TU), adb[:, c0 + i, d * 32:(d + 1) * 32])])
                         for i in range(cn) for d in range(2)], [self.mk, adb])
            self.act(ddt, ddt[:, c0:c0 + cn, :], ps, ps[:, 0:cn * 64].rearrange("p (a b) -> p a b", b=64), AF.Exp)
            ps2 = self.pmm.next()
            self.mm(ps2, [(ps2[:, i * 64:(i + 1) * 64], [(self.ones[:, :], adb[:, c0 + i, :])]) for i in range(cn)], [self.ones, adb])
            self.act(dtot, dtot[:, c0:c0 + cn, :], ps2, ps2[:, 0:cn * 64].rearrange("p (a b) -> p a b", b=64), AF.Exp)
        self.tt(ddt, ddt[:, :, :], ddt, ddt[:, :, :], dtk, dtk[:, :, :], ALU.mult)
        self.ck("dt2")

        for g in range(4):
            for (dst, bi) in ((BT, 16 + g), (CTt, 20 + g)):
                def cbc(ps, n0, nsz, dst=dst, bi=bi):
                    ct = ctmp.next()
                    self.conv(l, ps, n0, nsz, ct, ct[:, 0:nsz], "ccw", "ccb", bi, ll)
                    self.act(dst, dst[:, n0:n0 + nsz], ct, ct[:, 0:nsz], AF.Silu)
                self.win(l, BIDX["xbc"] + bi, cbc)
            for c0 in range(0, NCH, 8):
                cn = min(8, NCH - c0)
                self.tr(self.pT, [(self.pT[:, i * 128:(i + 1) * 128], BT[:, (c0 + i) * 128:(c0 + i + 1) * 128]) for i in range(cn)],
                        [BT, self.ident], self.ident[:, :])
                self.cp(Btok, Btok[:, c0:c0 + cn, :], self.pT, self.pT[:, 0:cn * 128].rearrange("p (a b) -> p a b", b=128), eng="act")
            self.ck("bc")
            for i in range(4):
                bi = g * 4 + i
                xs = xsf.next()

                def cxs(ps, n0, nsz, xs=xs, bi=bi):
                    ct = ctmp.next()
                    self.conv(l, ps, n0, nsz, ct, ct[:, 0:nsz], "ccw", "ccb", bi, ll)
                    self.act(xs, xs[:, n0:n0 + nsz], ct, ct[:, 0:nsz], AF.Silu)
                self.win(l, BIDX["xbc"] + bi, cxs)
                self.ts(bo, bo[:, i, :], xs, xs[:, :], self.ppc(l, "sdd", bi), None, ALU.mult, rd=[self.pp])
                for c0 in range(0, NCH, 8):
                    cn = min(8, NCH - c0)
                    self.tr(self.pT, [(self.pT[:, k * 128:(k + 1) * 128], xs[:, (c0 + k) * 128:(c0 + k + 1) * 128]) for k in range(cn)],
                            [xs, self.ident], self.ident[:, :])
                    self.cp(xtok, xtok[:, c0:c0 + cn, i * 128:(i + 1) * 128], self.pT,
                            self.pT[:, 0:cn * 128].rearrange("p (a b) -> p a b", b=128), eng=("act" if (c0 // 8) % 2 else "dve"))
            self.ck("xs")
            for d in range(2):
                P.op("dve", lambda e: e.memset(Sf[:, :, :], 0.0), [], [Sf])
                P.op("dve", lambda e: e.memset(Sb[:, :], 0.0), [], [Sb])
                order = list(range(NCH)) if d == 0 else [1, 0] + list(range(NCH - 1, 1, -1))
                U = self.UF if d == 0 else self.UB
                W = self.TL if d == 0 else self.TU
                hc = slice(d * 32 + g * 8, d * 32 + g * 8 + 8)
                for c in order:
                    cs = slice(c * 128, (c + 1) * 128)
                    R = Rr.next(); LT = LTr.next(); EC = ECr.next(); cbm = cbr.next(); xdt = xdr.next(); xdd = xdr.next()
                    pcb = self.pmm.next()
                    self.mm(pcb, [(pcb[:, 0:128], [(BT[:, cs], CTt[:, cs])])], [BT, CTt])
                    self.tt(cbm, cbm[:, :], pcb, pcb[:, 0:128], self.mk, U, ALU.mult)
                    self.tt(R, R[:, :, :], adk, adk[:, c, hc].unsqueeze(2).to_broadcast([128, 8, 128]),
                            self.mk, U.unsqueeze(1).to_broadcast([128, 8, 128]), ALU.mult)
                    Rf = R[:, :, :].rearrange("p a b -> p (a b)")
                    self.mm(self.pA, [(self.pA[:, hh * 512:(hh + 1) * 512], [(W, Rf[:, hh * 512:(hh + 1) * 512])]) for hh in range(2)],
                            [self.mk, R])
                    self.act(LT, LT[:, :, :], self.pA, self.pA[:, :].rearrange("p (a b) -> p a b", b=128), AF.Exp)
                    self.tt(LT, LT[:, :, :], LT, LT[:, :, :], cbm, cbm[:, :].unsqueeze(1).to_broadcast([128, 8, 128]), ALU.mult)
                    self.mm(self.pB, [(self.pB[:, hh * 512:(hh + 1) * 512], [(self.ones[:, :], Rf[:, hh * 512:(hh + 1) * 512])]) for hh in range(2)],
                            [self.ones, R])
                    self.act(EC, EC[:, :, :], self.pB, self.pB[:, :].rearrange("p (a b) -> p a b", b=128), AF.Exp)
                    self.tt(EC, EC[:, :, :], EC, EC[:, :, :], CTt, CTt[:, cs].unsqueeze(1).to_broadcast([128, 8, 128]), ALU.mult)
                    x3 = xtok[:, c, :].rearrange("p (a b) -> p a b", b=64)
                    self.tt(xdt, xdt[:, :, :], xtok, x3, dtk, dtk[:, c, hc].unsqueeze(2).to_broadcast([128, 8, 64]), ALU.mult)
                    py = self.pmm.next()
                    groups = []
                    for hp in range(4):
                        for e_ in range(2):
                            hh = 2 * hp + e_
                            groups.append((py[e_ * 64:(e_ + 1) * 64, hp * 128:(hp + 1) * 128],
                                           [(xdt[:, hh, :], LT[:, hh, :]), (Sb[:, hh * 64:(hh + 1) * 64], EC[:, hh, :])]))
                    self.mm(py, groups, [xdt, LT, Sb, EC])
                    self.tt(bo, bo[:, :, cs], py, py[:, :].rearrange("p (a b) -> p a b", b=128), bo, bo[:, :, cs], ALU.add)
                    self.tt(xdd, xdd[:, :, :], xtok, x3, ddt, ddt[:, c, hc].unsqueeze(2).to_broadcast([128, 8, 64]), ALU.mult)
                    pst = self.pmm.next()
                    self.mm(pst, [(pst[:, :], [(Btok[:, c, :], xdd[:, :, :].rearrange("p a b -> p (a b)"))])], [Btok, xdd])
                    self.tt(Sf, Sf[:, :, :], Sf, Sf[:, :, :], dtot, dtot[:, c, hc].unsqueeze(2).to_broadcast([128, 8, 64]), ALU.mult)
                    self.tt(Sf, Sf[:, :, :], Sf, Sf[:, :, :], pst, pst[:, :].rearrange("p (a b) -> p a b", b=64), ALU.add)
                    self.cp(Sb, Sb[:, :], Sf, Sf[:, :, :].rearrange("p a b -> p (a b)"), eng="act")
                    self.ck("scan1")
            self.ck("scan")
            for i in range(4):
                def cz(ps, n0, nsz, i=i):
                    gt = self.bring.next()
                    self.act(gt, gt[:, 0:nsz], ps, ps[:, 0:nsz], AF.Silu)
                    self.tt(bo, bo[:, i, n0:n0 + nsz], bo, bo[:, i, n0:n0 + nsz], gt, gt[:, 0:nsz], ALU.mult)
                self.win(l, BIDX["z"] + g * 4 + i, cz)
            for (n0, nsz) in NTILES:
                pss = self.pmm.next()
                sqs = []
                for i in range(4):
                    sq = xdr.next()
                    sqa = sq[:, :, :].rearrange("p a b -> p (a b)")[:, 0:nsz]
                    self.tt(sq, sqa, bo, bo[:, i, n0:n0 + nsz], bo, bo[:, i, n0:n0 + nsz], ALU.mult)
                    sqs.append((sq, sqa))
                self.mm(pss, [(pss[:, 0:nsz], [(self.ones[:, :], a_) for (_, a_) in sqs])], [s_ for (s_, _) in sqs] + [self.ones])
                self.ts(rs, rs[:, 0:nsz], pss, pss[:, 0:nsz], 1.0 / 512, EPS, ALU.mult, ALU.add)
                self.act(rs, rs[:, 0:nsz], rs, rs[:, 0:nsz], AF.Sqrt)
                P.op("dve", lambda e, nsz=nsz: e.reciprocal(rs[:, 0:nsz], rs[:, 0:nsz]), [rs], [rs])
                for i in range(4):
                    oa = bo[:, i, n0:n0 + nsz]
                    self.stt(bo, oa, bo, oa, self.ppc(l, "sng", g * 4 + i), rs, rs[:, 0:nsz], ALU.mult, ALU.mult, rd=[self.pp])
            self.ck("norm")
            for m in range(8):
                wt = self.pring.next()
                P.dma("pool", wt, wt[:, 0:4, :], self.wpc_d, self.wpc_d[l, m, :, g * 4:(g + 1) * 4, :])
                for (n0, nsz) in tiles:
                    pp_ = self.pmm.next()
                    self.mm(pp_, [(pp_[:, 0:nsz], [(wt[:, k, :], bo[:, k, n0:n0 + nsz]) for k in range(4)])], [wt, bo])
                    ma = self.merged[:, m, n0:n0 + nsz]
                    if g == 0:
                        self.cp(self.merged, ma, pp_, pp_[:, 0:nsz], eng="act")
                    else:
                        self.tt(self.merged, ma, pp_, pp_[:, 0:nsz], self.merged, ma, ALU.add)
        for m in range(8):
            def cg(ps, n0, nsz, m=m):
                gt = self.bring.next()
                self.act(gt, gt[:, 0:nsz], ps, ps[:, 0:nsz], AF.Sigmoid)
                ma = self.merged[:, m, n0:n0 + nsz]
                self.tt(self.merged, ma, self.merged, ma, gt, gt[:, 0:nsz], ALU.mult)
            self.win(l, BIDX["mg"] + 16 + m, cg, tiles)

    def final(self, l, b, skip_ctx):
        P, A = self.P, self.A
        A.reset()
        wo = A.view("wo", [128, 8, D], BF16)
        P.dma("pool", wo, wo[:, :, :], self.wout_d, self.wout_d[l])
        Gb = A.view("Gb", [128, D], F32); Gc = A.view("Gc", [128, D], F32)
        self.bload(Gb, l, b, 2); self.bload(Gc, l, 2, 2)
        xr = Ring([A.view("xf%d" % i, [128, D], F32) for i in range(3)])
        tr_ = Ring([A.view("tf%d" % i, [128, D], F32) for i in range(2)])
        junk = A.view("junkf", [128, D], BF16)
        for j in range(2 if skip_ctx else 0, NCH):
            xt = xr.next(); t = tr_.next(); st = self.stat.next()
            self.xdma(l, b, j, xt, xt[:, :], True, None)
            cs = slice(j * 128, (j + 1) * 128)
            self.mm(self.pA, [(self.pA[:, hh * 512:(hh + 1) * 512], [(self.merged[:, k, cs], wo[:, k, hh * 512:(hh + 1) * 512]) for k in range(8)])
                              for hh in range(2)], [self.merged, wo])
            P.op("dve", lambda e, st=st: e.memset(st[:, 0:1], 0.0), [], [st])
            self.act(junk, junk[:, :], self.pA, self.pA[:, :], AF.Square, accum=st[:, 0:1], wr=[st])
            self.rstd(st, 0, 1.0 / D)
            Ga = Gc if j < 2 else Gb
            self.stt(t, t[:, :], self.pA, self.pA[:, :], st[:, 0:1], Ga, Ga[:, :], ALU.mult, ALU.mult, rd=[st])
            self.tt(t, t[:, :], t, t[:, :], xt, xt[:, :], ALU.add)
            self.xdma(l, b, j, t, t[:, :], False, None)

    def build(self, stop=99, nb=2):
        try:
            self.build_(stop, nb)
        except StopBuild:
            pass
        P = self.P
        self.A.reset()
        for eng in ("sp",):
            w = P._waits(eng, [self.out_d], [self.out_d])
            P.ops[eng].append((w, None, None, 0))
        P.emit()
        return self.nc

    def build_(self, stop=99, nb=2):
        P = self.P
        self.setup()
        if stop >= 1:
            self.adaln()
        for l in range(self.nl):
            last = (l == self.nl - 1)
            tiles = NTILES[1:] if last else NTILES
            for b in range(nb):
                if stop >= 2:
                    self.phaseA(l, b)
                if stop >= 3:
                    self.ssd(l, b, tiles)
                if stop >= 4:
                    self.gla(l, b, tiles, "add")
                if stop >= 5:
                    self.lru(l, b, tiles, "add")
                if stop >= 6:
                    self.final(l, b, last)


def prep_shared(inp):
    f = lambda a: np.ascontiguousarray(np.asarray(a, dtype=np.float32))
    w_in = f(inp["w_in"])
    win = np.zeros((L, NBLK, 128, 8, 128), np.float32)
    for i, (c0, m) in enumerate(BLKS):
        win[:, i, :, :, :m] = w_in[:, :, c0:c0 + m].reshape(L, 8, 128, m).transpose(0, 2, 1, 3)
    sh = {"w_in": win}
    sh["ada_w"] = f(f(inp["ada_w"]).reshape(L, 8, 128, 3072).transpose(0, 2, 1, 3))
    rows = np.zeros((L, 3, 3072), np.float32)
    rows[:, 0, :] = f(inp["ada_b"]); rows[:, 1, :1024] = f(inp["pre_g"]); rows[:, 2, :1024] = f(inp["post_g"])
    rows[:, 1, 1024:1088] = f(inp["ssd_dt_bias"]).reshape(L, 64); rows[:, 1, 1088:1152] = f(inp["ssd_a_log"]).reshape(L, 64)
    sh["rows"] = rows
    pp = np.zeros((128, L, PPW), np.float32)

    def put(name, arr):
        n = arr.shape[1]
        pp[:, :, PPL[name]:PPL[name] + n] = arr.transpose(2, 0, 1)
    put("caw", f(inp["conv_a_w"]).reshape(L, 4, 8, 128).transpose(0, 2, 1, 3).reshape(L, 32, 128))
    put("cab", f(inp["conv_a_b"]).reshape(L, 8, 128))
    put("lbr", f(inp["lru_br"]).reshape(L, 16, 128))
    put("lbi", f(inp["lru_bi"]).reshape(L, 16, 128))
    put("llam", f(inp["lru_lam"]).reshape(L, 16, 128))
    put("gab", f(inp["gla_alpha_b"]).reshape(L, 8, 128))
    put("gng", f(inp["gla_norm_g"]).reshape(L, 2, 128))
    put("ccw", f(inp["conv_c_w"]).reshape(L, 4, 24, 128).transpose(0, 2, 1, 3).reshape(L, 96, 128))
    put("ccb", f(inp["conv_c_b"]).reshape(L, 24, 128))
    put("sdd", np.repeat(f(inp["ssd_d"]).reshape(L, 16, 2), 64, axis=2))
    put("sng", f(inp["ssd_norm_g"]).reshape(L, 16, 128))
    z64 = np.zeros((L, 1, 64), np.float32)
    put("sal", np.concatenate([f(inp["ssd_a_log"]).reshape(L, 1, 64), z64], 2))
    put("sdb", np.concatenate([f(inp["ssd_dt_bias"]).reshape(L, 1, 64), z64], 2))
    sh["pp"] = np.ascontiguousarray(pp.reshape(128, NPP))
    wr = f(inp["lru_wr"]); wi = f(inp["lru_wi"])
    lw = np.stack([wr, wi], 1)
    sh["lruw"] = np.ascontiguousarray(lw.transpose(0, 3, 4, 1, 2, 5).reshape(L, 8, 128, 4, 128))
    sh["gup"] = np.ascontiguousarray(f(inp["gla_alpha_up"]).transpose(0, 2, 1, 3))
    for nm, key, nk in (("wpa", "w_pa", 8), ("wpb", "w_pb", 8), ("wpc", "w_pc", 16)):
        w = f(inp[key]).reshape(L, nk, 128, 8, 128)
        sh[nm] = np.ascontiguousarray(w.transpose(0, 3, 2, 1, 4))
    sh["wout"] = np.ascontiguousarray(f(inp["w_out"]).reshape(L, 8, 128, 1024).transpose(0, 2, 1, 3))
    return sh


def core_inputs(inp, sh, i):
    f = lambda a: np.ascontiguousarray(np.asarray(a, dtype=np.float32))
    m = dict(sh)
    m["x"] = f(inp["x"][2 * i:2 * i + 2])
    m["ctx"] = f(inp["ctx"][2 * i:2 * i + 2])
    crow = np.stack([f(inp["c"][2 * i]), f(inp["c"][2 * i + 1]), f(inp["c_ctx"])], 0)
    m["cT"] = np.ascontiguousarray(crow.reshape(3, 8, 128).transpose(2, 1, 0))
    return m


_NC = {}


def kernel(**inputs):
    if "nc" not in _NC:
        _NC["nc"] = K(L).build()
    sh = prep_shared(inputs)
    in_maps = [core_inputs(inputs, sh, i) for i in range(8)]
    res = run_bass_kernel_spmd(_NC["nc"], in_maps, core_ids=list(range(8)))
    return np.concatenate([r["out"] for r in res.results], axis=0).astype(np.float32)
```

```python
from contextlib import ExitStack
import numpy as np
import concourse.bass as bass
import concourse.mybir as mybir
from concourse.bass_utils import run_bass_kernel_spmd

F32 = mybir.dt.float32
BF16 = mybir.dt.bfloat16
AF = mybir.ActivationFunctionType
ALU = mybir.AluOpType

L = 4
D = 1024
S = 2048
CT = 256
T = CT + S
NCH = T // 128
EPS = 1e-6
NTILES = [(0, 256), (256, 512), (768, 512), (1280, 512), (1792, 512)]
O_LX, O_LG, O_Q, O_K, O_V, O_GG, O_LOW, O_Z, O_XBC, O_DT, O_MG = (
    0, 1024, 2048, 2560, 3072, 4096, 5120, 5152, 7200, 10272, 10336)
BLKS = []
BIDX = {}


def _mkblks():
    def add(name, c0, n, m=128):
        BIDX[name] = len(BLKS)
        for i in range(n):
            BLKS.append((c0 + i * m, m))
    add("lx", O_LX, 8); add("lg", O_LG, 8); add("q", O_Q, 4); add("k", O_K, 4)
    add("v", O_V, 8); add("gg", O_GG, 8); add("low", O_LOW, 2, 16); add("z", O_Z, 16)
    add("xbc", O_XBC, 24); add("dt", O_DT, 1, 64); add("mg", O_MG, 24)


_mkblks()
NBLK = len(BLKS)

PPL = {}
_o = 0
for _n, _w in (("caw", 32), ("cab", 8), ("lbr", 16), ("lbi", 16), ("llam", 16), ("gab", 8), ("gng", 2),
               ("ccw", 96), ("ccb", 24), ("sdd", 16), ("sng", 16), ("sal", 1), ("sdb", 1)):
    PPL[_n] = _o
    _o += _w
PPW = _o
NPP = PPW * L


class Tk:
    def __init__(self, name, h, semkey=None):
        self.name = name
        self.h = h
        self.last_w = None
        self.readers = []
        self.semkey = semkey or ("D_" + name)

    def __getitem__(self, idx):
        return self.h[idx]


class Prog:
    ENG = ("pe", "act", "dve", "pool", "sp")

    def __init__(self, nc):
        self.nc = nc
        self.es = ExitStack()
        self.ops = {e: [] for e in self.ENG}
        self.cnt = {e: 0 for e in self.ENG}
        self.waited = {e: {} for e in self.ENG}
        self.sems = {}
        self.dcount = {}
        for e in self.ENG:
            self._sem("E_" + e)

    def _sem(self, key):
        if key not in self.sems:
            self.sems[key] = self.es.enter_context(self.nc.semaphore("s%d" % len(self.sems)))
        return self.sems[key]

    def sbuf(self, name, shape, dtype):
        return Tk(name, self.es.enter_context(self.nc.sbuf_tensor("sb_" + name, list(shape), dtype)))

    def psum(self, name, shape, dtype=F32):
        return Tk(name, self.es.enter_context(self.nc.psum_tensor("ps_" + name, list(shape), dtype)))

    def dram(self, name, shape, dtype, kind="Internal"):
        return Tk(name, self.nc.dram_tensor(name, list(shape), dtype, kind=kind))

    def _waits(self, eng, reads, writes):
        evs = []
        for t in reads:
            if t.last_w is not None:
                evs.append(t.last_w)
        for t in writes:
            if t.last_w is not None:
                evs.append(t.last_w)
            evs.extend(t.readers)
        w = {}
        wd = self.waited[eng]
        for (k, v) in evs:
            if wd.get(k, 0) >= v:
                continue
            if w.get(k, 0) < v:
                w[k] = v
        for k, v in w.items():
            wd[k] = v
        return list(w.items())

    def _commit(self, ev, reads, writes):
        for t in reads:
            t.readers.append(ev)
        for t in writes:
            t.last_w = ev
            t.readers = []

    def op(self, eng, fn, reads=(), writes=()):
        waits = self._waits(eng, reads, writes)
        self.cnt[eng] += 1
        ev = ("E_" + eng, self.cnt[eng])
        self.ops[eng].append((waits, fn, ev[0], 1))
        self._commit(ev, reads, writes)

    def dma(self, q, out_t, out_ap, in_t, in_ap):
        key = out_t.semkey
        self._sem(key)
        waits = self._waits(q, [in_t], [out_t])
        self.dcount[key] = self.dcount.get(key, 0) + 16
        ev = (key, self.dcount[key])

        def fn(e, out_ap=out_ap, in_ap=in_ap):
            return e.dma_start(out=out_ap, in_=in_ap)
        self.ops[q].append((waits, fn, key, 16))
        self._commit(ev, [in_t], [out_t])

    def barrier(self, tiles):
        for eng in self.ENG:
            waits = self._waits(eng, [], tiles)
            if waits:
                self.ops[eng].append((waits, None, None, 0))

    def emit(self):
        nc = self.nc
        sems = self.sems
        ops = self.ops

        def run(e, lst):
            for (waits, fn, sk, inc) in lst:
                for (k, v) in waits:
                    e.wait_ge(sems[k], v)
                if fn is not None:
                    fn(e).then_inc(sems[sk], inc)

        with nc.Block() as block:
            @block.tensor
            def _(e):
                run(e, ops["pe"])

            @block.scalar
            def _(e):
                run(e, ops["act"])

            @block.vector
            def _(e):
                run(e, ops["dve"])

            @block.gpsimd
            def _(e):
                run(e, ops["pool"])

            @block.sync
            def _(e):
                run(e, ops["sp"])
        self.es.close()


class Ring:
    def __init__(self, tiles):
        self.t = tiles
        self.i = 0

    def next(self):
        t = self.t[self.i % len(self.t)]
        self.i += 1
        return t


class Arena:
    def __init__(self, P, nbytes):
        self.P = P
        self.t = P.sbuf("arena", [128, nbytes // 4], F32)
        self.nbytes = nbytes
        self.off = 0
        self.live = []
        self.gen = 0

    def reset(self):
        self.P.barrier(self.live)
        self.live = []
        self.off = 0
        self.gen += 1

    def view(self, name, shape, dtype, semkey=None):
        esz = 4 if dtype == F32 else 2
        n = 1
        for s in shape[1:]:
            n *= s
        nb = (n * esz + 31) // 32 * 32
        assert self.off + nb <= self.nbytes, (name, self.off, nb, self.nbytes)
        a = self.t.h[0:shape[0], self.off // 4:(self.off + nb) // 4]
        if dtype != F32:
            a = a.bitcast(dtype)
        a = a[:, 0:n]
        if len(shape) == 3:
            a = a.rearrange("p (a b) -> p a b", b=shape[2])
        elif len(shape) == 4:
            a = a.rearrange("p (a b c) -> p a b c", b=shape[2], c=shape[3])
        self.off += nb
        t = Tk("%s_g%d" % (name, self.gen), a, semkey=semkey or ("DA_" + name))
        self.live.append(t)
        return t


class StopBuild(Exception):
    pass


class K:
    def ck(self, tag):
        import os
        if os.environ.get("CK") == tag:
            raise StopBuild(tag)

    def __init__(self, nl=L):
        self.nl = nl
        nc = bass.Bass("TRN2", target_bir_lowering=False)
        self.nc = nc
        self.P = P = Prog(nc)
        d = P.dram
        self.x_d = d("x", [2, S, D], F32, "ExternalInput")
        self.ctx_d = d("ctx", [2, CT, D], F32, "ExternalInput")
        self.cT_d = d("cT", [128, 8, 3], F32, "ExternalInput")
        self.adaw_d = d("ada_w", [L, 128, 8, 3072], F32, "ExternalInput")
        self.rows_d = d("rows", [L, 3, 3072], F32, "ExternalInput")
        self.win_d = d("w_in", [L, NBLK, 128, 8, 128], F32, "ExternalInput")
        self.pp_d = d("pp", [128, NPP], F32, "ExternalInput")
        self.lruw_d = d("lruw", [L, 8, 128, 4, 128], F32, "ExternalInput")
        self.gup_d = d("gup", [L, 16, 2, 512], F32, "ExternalInput")
        self.wpa_d = d("wpa", [L, 8, 128, 8, 128], F32, "ExternalInput")
        self.wpb_d = d("wpb", [L, 8, 128, 8, 128], F32, "ExternalInput")
        self.wpc_d = d("wpc", [L, 8, 128, 16, 128], F32, "ExternalInput")
        self.wout_d = d("wout", [L, 128, 8, 1024], F32, "ExternalInput")
        self.out_d = d("out", [2, S, D], F32, "ExternalOutput")
        self.ctxs_d = d("ctxs", [2, CT, D], F32)
        self.mods_d = d("mods", [L, 3, 3, 1024], F32)

        self.hT = P.sbuf("hT", [128, 8, T], BF16)
        self.merged = P.sbuf("merged", [128, 8, T], BF16)
        self.pp = P.sbuf("pp", [128, NPP], F32)
        self.pd = P.sbuf("pd", [128, L * 32], F32)
        self.identf = P.sbuf("identf", [128, 128], F32)
        self.ident = P.sbuf("ident", [128, 128], BF16)
        self.mf = P.sbuf("mf", [128, 4, 128], F32)
        self.mk = P.sbuf("mk", [128, 4, 128], BF16)
        self.ones = P.sbuf("ones", [128, 128], BF16)
        self.onef = P.sbuf("onef", [128, 1], F32)
        self.stat = Ring([P.sbuf("stat%d" % i, [128, 4], F32) for i in range(4)])
        self.wring = Ring([P.sbuf("wblk%d" % i, [128, 8, 128], BF16) for i in range(4)])
        self.pring = Ring([P.sbuf("pblk%d" % i, [128, 8, 128], BF16) for i in range(2)])
        self.lring = Ring([P.sbuf("lblk%d" % i, [128, 4, 128], BF16) for i in range(2)])
        self.gup = P.sbuf("gup", [16, 2, 512], BF16)
        self.bring = Ring([P.sbuf("btmp%d" % i, [128, 512], BF16) for i in range(3)])
        used = (2 * 8 * T * 2 + NPP * 4 + L * 32 * 4 + 512 + 256 + 2048 + 1024 + 256 + 4 + 4 * 16
                + 4 * 2048 + 2 * 2048 + 2 * 1024 + 2048 + 3 * 1024)
        self.A = Arena(P, (211900 - used) // 32 * 32)
        self.pmm = Ring([P.psum("pmm%d" % i, [128, 512]) for i in range(3)])
        self.pA = P.psum("pA", [128, 1024])
        self.pB = P.psum("pB", [128, 1024])
        self.pT = P.psum("pT", [128, 1024], BF16)

    def act(self, ot, oap, it, iap, func, bias=None, scale=None, accum=None, rd=(), wr=()):
        kw = {}
        if bias is not None:
            kw["bias"] = bias
        if scale is not None:
            kw["scale"] = scale
        if accum is not None:
            kw["accum_out"] = accum
        self.P.op("act", lambda e: e.activation(oap, iap, func, **kw), [it] + list(rd), [ot] + list(wr))

    def tt(self, ot, oap, at, aap, bt, bap, op, eng="dve"):
        self.P.op(eng, lambda e: e.tensor_tensor(oap, aap, bap, op), [at, bt], [ot])

    def ts(self, ot, oap, at, aap, s1, s2, op0, op1=None, rd=(), eng="dve"):
        if op1 is None:
            self.P.op(eng, lambda e: e.tensor_scalar(oap, aap, s1, None, op0), [at] + list(rd), [ot])
        else:
            self.P.op(eng, lambda e: e.tensor_scalar(oap, aap, s1, s2, op0, op1), [at] + list(rd), [ot])

    def stt(self, ot, oap, at, aap, sc, bt, bap, op0, op1, rd=(), eng="dve"):
        self.P.op(eng, lambda e: e.scalar_tensor_tensor(oap, aap, sc, bap, op0, op1), [at, bt] + list(rd), [ot])

    def cp(self, ot, oap, it, iap, eng="dve"):
        if eng == "act":
            self.P.op("act", lambda e: e.activation(oap, iap, AF.Copy), [it], [ot])
        else:
            self.P.op(eng, lambda e: e.tensor_copy(oap, iap), [it], [ot])

    def mm(self, ot, groups, reads):
        def fn(e):
            ins = None
            for (oap, pairs) in groups:
                n = len(pairs)
                for i, (l_, r_) in enumerate(pairs):
                    ins = e.matmul(oap, l_, r_, start=(i == 0), stop=(i == n - 1))
            return ins
        self.P.op("pe", fn, reads, [ot])

    def tr(self, ot, items, reads, ident):
        def fn(e):
            ins = None
            for (oap, iap) in items:
                ins = e.transpose(oap, iap, ident)
            return ins
        self.P.op("pe", fn, reads, [ot])

    def rstd(self, st, col, n_inv):
        a = st[:, col:col + 1]
        self.ts(st, a, st, a, n_inv, EPS, ALU.mult, ALU.add)
        self.act(st, a, st, a, AF.Sqrt)
        self.P.op("dve", lambda e: e.reciprocal(a, a), [st], [st])

    def ppc(self, l, name, i=0):
        c = l * PPW + PPL[name] + i
        return self.pp[:, c:c + 1]

    def win(self, l, bi, consume, tiles=NTILES):
        c0, M = BLKS[bi]
        wt = self.wring.next()
        self.P.dma("pool", wt, wt[:, :, :], self.win_d, self.win_d[l, bi])
        for (n0, nsz) in tiles:
            ps = self.pmm.next()
            self.mm(ps, [(ps[0:M, 0:nsz], [(wt[:, k, 0:M], self.hT[:, k, n0:n0 + nsz]) for k in range(8)])],
                    [wt, self.hT])
            consume(ps, n0, nsz)

    def conv(self, l, ps, n0, nsz, dst_t, dst_ap, wname, bname, blk, ll):
        if n0 == 0:
            ll = nsz
        w = [self.ppc(l, wname, blk * 4 + k) for k in range(4)]
        self.act(dst_t, dst_ap, ps, ps[:, 0:nsz], AF.Identity, bias=self.ppc(l, bname, blk), scale=w[1], rd=[self.pp])
        d3 = dst_ap.rearrange("p (a b) -> p a b", b=ll)
        p3 = ps[:, 0:nsz].rearrange("p (a b) -> p a b", b=ll)
        for (k, so, do_) in ((0, slice(0, ll - 1), slice(1, ll)), (2, slice(1, ll), slice(0, ll - 1)),
                             (3, slice(2, ll), slice(0, ll - 2))):
            self.stt(dst_t, d3[:, :, do_], ps, p3[:, :, so], w[k], dst_t, d3[:, :, do_], ALU.mult, ALU.add,
                     rd=[self.pp])

    def setup(self):
        P = self.P
        P.dma("sp", self.pp, self.pp[:, :], self.pp_d, self.pp_d[:, :])
        P.op("pool", lambda e: e.memset(self.identf[:, :], 0.0), [], [self.identf])
        P.op("pool", lambda e: e.affine_select(self.identf[:, :], self.identf[:, :], [[-1, 128]], ALU.not_equal,
                                               1.0, base=0, channel_multiplier=1), [self.identf], [self.identf])
        self.cp(self.ident, self.ident[:, :], self.identf, self.identf[:, :])
        P.op("pool", lambda e: e.memset(self.mf[:, :, :], 1.0), [], [self.mf])
        for i, (pat, cm, cmp_) in enumerate((([[-1, 128]], 1, ALU.is_gt), ([[1, 128]], -1, ALU.is_gt),
                                             ([[1, 128]], -1, ALU.is_ge), ([[-1, 128]], 1, ALU.is_ge))):
            P.op("pool", lambda e, i=i, pat=pat, cm=cm, cmp_=cmp_: e.affine_select(
                self.mf[:, i, :], self.mf[:, i, :], pat, cmp_, 0.0, base=0, channel_multiplier=cm),
                [self.mf], [self.mf])
        self.cp(self.mk, self.mk[:, :, :], self.mf, self.mf[:, :, :])
        P.op("dve", lambda e: e.memset(self.ones[:, :], 1.0), [], [self.ones])
        P.op("dve", lambda e: e.memset(self.onef[:, :], 1.0), [], [self.onef])
        for l in range(L):
            b = l * 32
            lam = self.pp[:, l * PPW + PPL["llam"]: l * PPW + PPL["llam"] + 16]
            o = self.pd[:, b:b + 16]
            self.act(self.pd, o, self.pp, lam, AF.Exp, scale=-1.0)
            self.act(self.pd, o, self.pd, o, AF.Ln, bias=1.0)
            self.ts(self.pd, o, self.pd, o, -8.0, None, ALU.mult)
            gb = self.pp[:, l * PPW + PPL["gab"]: l * PPW + PPL["gab"] + 8]
            self.ts(self.pd, self.pd[:, b + 16:b + 24], self.pp, gb, -1.0, None, ALU.mult)
            al = self.ppc(l, "sal")
            self.act(self.pd, self.pd[:, b + 24:b + 25], self.pp, al, AF.Exp)
            self.ts(self.pd, self.pd[:, b + 24:b + 25], self.pd, self.pd[:, b + 24:b + 25], -1.0, None, ALU.mult)
        self.TL, self.TU, self.UF, self.UB = (self.mk[:, i, :] for i in range(4))

    def adaln(self):
        P, A = self.P, self.A
        A.reset()
        scT = A.view("scT", [128, 8, 3], F32)
        P.dma("sp", scT, scT[:, :, :], self.cT_d, self.cT_d[:, :, :])
        self.act(scT, scT[:, :, :], scT, scT[:, :, :], AF.Silu)
        awr = Ring([A.view("adaw%d" % i, [128, 8, 512], F32) for i in range(2)])
        rows = [A.view("rows%d" % i, [3, 3072], F32) for i in range(3)]
        modr = A.view("modr", [3, 3072], F32)
        modA = A.view("modA", [3, 2048], F32)
        for l in range(self.nl):
            for i in range(3):
                P.dma("sp", rows[i], rows[i][:, :], self.rows_d, self.rows_d[l, i:i + 1, :].partition_broadcast(3))
            for n in range(6):
                aw = awr.next()
                P.dma("sp", aw, aw[:, :, :], self.adaw_d, self.adaw_d[l, :, :, n * 512:(n + 1) * 512])
                ps = self.pmm.next()
                self.mm(ps, [(ps[0:3, :], [(scT[:, k, :], aw[:, k, :]) for k in range(8)])], [scT, aw])
                self.tt(modr, modr[:, n * 512:(n + 1) * 512], ps, ps[0:3, :], rows[0], rows[0][:, n * 512:(n + 1) * 512],
                        ALU.add)
            self.stt(modA, modA[:, 0:1024], modr, modr[:, 1024:2048], 1.0, rows[1], rows[1][:, 0:1024], ALU.add, ALU.mult)
            self.tt(modA, modA[:, 1024:2048], modr, modr[:, 2048:3072], rows[2], rows[2][:, 0:1024], ALU.mult)
            P.dma("sp", self.mods_d, self.mods_d[l, :, 0, :], modA, modA[:, 0:1024])
            P.dma("sp", self.mods_d, self.mods_d[l, :, 1, :], modr, modr[:, 0:1024])
            P.dma("sp", self.mods_d, self.mods_d[l, :, 2, :], modA, modA[:, 1024:2048])

    def xdma(self, l, b, j, sb_t, sb_ap, load, first_layer_src):
        P = self.P
        if j < 2:
            dt_ = (self.ctx_d if (load and l == 0) else self.ctxs_d)
            aps = [(dt_[b, j * 128:(j + 1) * 128, :], sb_ap)]
        else:
            dt_ = (self.x_d if (load and l == 0) else self.out_d)
            jj = j - 2
            if l % 2 == 0:
                aps = [(dt_[b, jj * 128:(jj + 1) * 128, :], sb_ap)]
            else:
                v = dt_[b].rearrange("(r w) d -> w r d", w=64)
                aps = [(v[jj * 4 + wl], sb_ap[wl * 32:(wl + 1) * 32, :]) for wl in range(4)]
        for (da, sa) in aps:
            if load:
                P.dma("sp", sb_t, sa, dt_, da)
            else:
                P.dma("sp", dt_, da, sb_t, sa)

    def bload(self, t, l, row, which):
        self.P.dma("sp", t, t[:, :], self.mods_d, self.mods_d[l, row, which:which + 1, :].partition_broadcast(128))

    def phaseA(self, l, b):
        P, A = self.P, self.A
        A.reset()
        Ab = A.view("Ab", [128, D], F32); Sb = A.view("Sb", [128, D], F32)
        Ac = A.view("Ac", [128, D], F32); Sc = A.view("Sc", [128, D], F32)
        self.bload(Ab, l, b, 0); self.bload(Sb, l, b, 1); self.bload(Ac, l, 2, 0); self.bload(Sc, l, 2, 1)
        xr = Ring([A.view("xa%d" % i, [128, D], F32) for i in range(3)])
        t1r = Ring([A.view("t1_%d" % i, [128, D], F32) for i in range(2)])
        hbr = Ring([A.view("hb%d" % i, [128, D], BF16) for i in range(2)])
        junk = A.view("junk", [128, D], BF16)
        for j in range(NCH):
            xt = xr.next(); t1 = t1r.next(); hb = hbr.next(); st = self.stat.next()
            self.xdma(l, b, j, xt, xt[:, :], True, None)
            P.op("dve", lambda e, st=st: e.memset(st[:, 0:1], 0.0), [], [st])
            self.act(junk, junk[:, :], xt, xt[:, :], AF.Square, accum=st[:, 0:1], wr=[st])
            self.rstd(st, 0, 1.0 / D)
            Aa, Sa = (Ac, Sc) if j < 2 else (Ab, Sb)
            self.stt(t1, t1[:, :], xt, xt[:, :], st[:, 0:1], Aa, Aa[:, :], ALU.mult, ALU.mult, rd=[st])
            self.tt(hb, hb[:, :], t1, t1[:, :], Sa, Sa[:, :], ALU.add)
            self.tr(self.pT, [(self.pT[:, k * 128:(k + 1) * 128], hb[:, k * 128:(k + 1) * 128]) for k in range(8)],
                    [hb, self.ident], self.ident[:, :])
            self.cp(self.hT, self.hT[:, :, j * 128:(j + 1) * 128],
                    self.pT, self.pT[:, :].rearrange("p (a b) -> p a b", b=128), eng=("act" if j % 2 else "dve"))

    def project(self, l, bo_t, bo_ap, nk, w_d, gi, mode, tiles):
        P = self.P
        for m in range(8):
            wts = []
            for k0 in range(0, nk, 8):
                wt = self.pring.next()
                P.dma("pool", wt, wt[:, :, :], w_d, w_d[l, m, :, k0:k0 + 8, :])
                wts.append(wt)
            gt = self.wring.next()
            bi = BIDX["mg"] + gi * 8 + m
            P.dma("pool", gt, gt[:, :, :], self.win_d, self.win_d[l, bi])
            for (n0, nsz) in tiles:
                pg = self.pmm.next()
                self.mm(pg, [(pg[:, 0:nsz], [(gt[:, k, :], self.hT[:, k, n0:n0 + nsz]) for k in range(8)])], [gt, self.hT])
                g = self.bring.next()
                self.act(g, g[:, 0:nsz], pg, pg[:, 0:nsz], AF.Sigmoid)
                pp_ = self.pmm.next()
                self.mm(pp_, [(pp_[:, 0:nsz], [(wts[k // 8][:, k % 8, :], bo_ap[:, k, n0:n0 + nsz]) for k in range(nk)])],
                        wts + [bo_t])
                ma = self.merged[:, m, n0:n0 + nsz]
                if mode == "set":
                    self.tt(self.merged, ma, pp_, pp_[:, 0:nsz], g, g[:, 0:nsz], ALU.mult)
                else:
                    t = self.bring.next()
                    self.tt(t, t[:, 0:nsz], pp_, pp_[:, 0:nsz], g, g[:, 0:nsz], ALU.mult)
                    self.tt(self.merged, ma, self.merged, ma, t, t[:, 0:nsz], ALU.add)

    def lru(self, l, b, tiles, mode):
        P, A = self.P, self.A
        A.reset()
        ll = 32 if l % 2 else 64
        ya = A.view("ya", [128, 8, T], BF16)
        xa = A.view("xa", [128, T], F32); xab = A.view("xab", [128, T], BF16)
        rt = A.view("rt", [128, T], F32); it = A.view("it", [128, T], F32); tmp = A.view("ltmp", [128, T], F32)
        hf = A.view("hf", [128, T], F32); hbk = A.view("hbk", [128, T], F32)
        for n in range(8):
            lw = self.lring.next()
            P.dma("pool", lw, lw[:, :, :], self.lruw_d, self.lruw_d[l, n])
            self.win(l, BIDX["lx"] + n, lambda ps, n0, nsz: self.conv(l, ps, n0, nsz, xa, xa[:, n0:n0 + nsz], "caw", "cab", n, ll))
            self.cp(xab, xab[:, :], xa, xa[:, :], eng="act")
            for d in range(2):
                for (n0, nsz) in NTILES:
                    for (ri, dst, bn) in ((0, rt, "lbr"), (1, it, "lbi")):
                        ps = self.pmm.next()
                        self.mm(ps, [(ps[:, 0:nsz], [(lw[:, ri * 2 + d, :], xab[:, n0:n0 + nsz])])], [lw, xab])
                        self.act(dst, dst[:, n0:n0 + nsz], ps, ps[:, 0:nsz], AF.Sigmoid, bias=self.ppc(l, bn, d * 8 + n),
                                 rd=[self.pp])
                cl = self.pd[:, l * 32 + d * 8 + n: l * 32 + d * 8 + n + 1]
                self.act(rt, rt[:, :], rt, rt[:, :], AF.Exp, scale=cl, rd=[self.pd])
                self.act(tmp, tmp[:, :], rt, rt[:, :], AF.Square)
                self.ts(tmp, tmp[:, :], tmp, tmp[:, :], -1.0, 1.0, ALU.mult, ALU.add)
                self.ts(tmp, tmp[:, :], tmp, tmp[:, :], 0.0, None, ALU.max)
                self.act(tmp, tmp[:, :], tmp, tmp[:, :], AF.Sqrt)
                self.tt(it, it[:, :], it, it[:, :], xa, xa[:, :], ALU.mult)
                self.tt(it, it[:, :], it, it[:, :], tmp, tmp[:, :], ALU.mult)
                if d == 0:
                    P.op("dve", lambda e: e.tensor_tensor_scan(hf[:, :], rt[:, :], it[:, :], 0.0, ALU.mult, ALU.add),
                         [rt, it], [hf])
                else:
                    P.op("dve", lambda e: e.tensor_tensor_scan(hbk[:, CT - 1::-1], rt[:, CT - 1::-1], it[:, CT - 1::-1],
                                                               0.0, ALU.mult, ALU.add), [rt, it], [hbk])
                    P.op("dve", lambda e: e.tensor_tensor_scan(hbk[:, T - 1:CT - 1:-1], rt[:, T - 1:CT - 1:-1],
                                                               it[:, T - 1:CT - 1:-1], hbk[:, 0:1], ALU.mult, ALU.add),
                         [rt, it, hbk], [hbk])
            self.tt(hf, hf[:, :], hf, hf[:, :], hbk, hbk[:, :], ALU.add)

            def cons(ps, n0, nsz, n=n):
                g = self.bring.next()
                self.act(g, g[:, 0:nsz], ps, ps[:, 0:nsz], AF.Silu)
                self.tt(ya, ya[:, n, n0:n0 + nsz], hf, hf[:, n0:n0 + nsz], g, g[:, 0:nsz], ALU.mult)
            self.win(l, BIDX["lg"] + n, cons)
        self.project(l, ya, ya, 8, self.wpa_d, 0, mode, tiles)

    def gla(self, l, b, tiles, mode):
        P, A = self.P, self.A
        A.reset()
        ob = A.view("ob", [128, 8, T], BF16)
        low = [A.view("low%d" % d, [16, T], BF16) for d in range(2)]
        vtok = A.view("vtok", [128, NCH, 256], BF16)
        qT = A.view("qT", [128, T], BF16); kT = A.view("kT", [128, T], BF16)
        qe = A.view("qe", [128, T], BF16); ke = A.view("ke", [128, T], BF16); kd = A.view("kd", [128, T], BF16)
        tA = A.view("tA", [128, T], F32); tB = A.view("tB", [128, T], F32); G = A.view("G", [128, T + 1], F32)
        vTa = tB.h.bitcast(BF16)[:, 0:T]
        ebt = A.view("ebt", [128, NCH], F32)
        Sf = A.view("Sf", [128, 256], F32); Sb = A.view("Sbf", [128, 256], BF16)
        kdt = Ring([A.view("kdt%d" % i, [128, 128], BF16) for i in range(2)])
        atm = Ring([A.view("atm%d" % i, [128, 128], BF16) for i in range(2)])
        P.dma("pool", self.gup, self.gup[:, :, :], self.gup_d, self.gup_d[l])
        for d in range(2):
            self.win(l, BIDX["low"] + d, lambda ps, n0, nsz, d=d: self.cp(low[d], low[d][:, n0:n0 + nsz], ps, ps[0:16, 0:nsz], eng="act"))
        for h in range(4):
            self.win(l, BIDX["q"] + h, lambda ps, n0, nsz: self.act(qT, qT[:, n0:n0 + nsz], ps, ps[:, 0:nsz], AF.Copy, scale=128.0 ** -0.5))
            self.win(l, BIDX["k"] + h, lambda ps, n0, nsz: self.cp(kT, kT[:, n0:n0 + nsz], ps, ps[:, 0:nsz], eng="act"))
            for j in range(2):
                self.win(l, BIDX["v"] + h * 2 + j, lambda ps, n0, nsz: self.cp(tB, vTa[:, n0:n0 + nsz], ps, ps[:, 0:nsz], eng="act"))
                for c0 in range(0, NCH, 8):
                    cn = min(8, NCH - c0)
                    self.tr(self.pT, [(self.pT[:, i * 128:(i + 1) * 128], vTa[:, (c0 + i) * 128:(c0 + i + 1) * 128]) for i in range(cn)],
                            [tB, self.ident], self.ident[:, :])
                    self.cp(vtok, vtok[:, c0:c0 + cn, j * 128:(j + 1) * 128],
                            self.pT, self.pT[:, 0:cn * 128].rearrange("p (a b) -> p a b", b=128))
            obh = ob[:, 2 * h:2 * h + 2, :]
            for d in range(2):
                nb = self.pd[:, l * 32 + 16 + d * 4 + h: l * 32 + 16 + d * 4 + h + 1]
                for (n0, nsz) in NTILES:
                    ps = self.pmm.next()
                    self.mm(ps, [(ps[:, 0:nsz], [(self.gup[:, d, h * 128:(h + 1) * 128], low[d][:, n0:n0 + nsz])])], [self.gup, low[d]])
                    self.act(tA, tA[:, n0:n0 + nsz], ps, ps[:, 0:nsz], AF.Exp, bias=nb, scale=-1.0, rd=[self.pd])
                self.act(tA, tA[:, :], tA, tA[:, :], AF.Ln, bias=1.0)
                onesb = self.onef[:, 0:1].to_broadcast([128, T])
                if d == 0:
                    P.op("dve", lambda e: e.memset(G[:, 0:1], 0.0), [], [G])
                    P.op("dve", lambda e: e.tensor_tensor_scan(G[:, 1:T + 1], onesb, tA[:, :], 0.0, ALU.mult, ALU.add), [tA, self.onef], [G])
                    hi = G[:, 1:T + 1].rearrange("p (c j) -> p c j", j=128)
                    lo = G[:, 0:T:128].unsqueeze(2).to_broadcast([128, NCH, 128])
                    tot = tA[:, 127:T:128]
                else:
                    P.op("dve", lambda e: e.memset(G[:, T:T + 1], 0.0), [], [G])
                    P.op("dve", lambda e: e.tensor_tensor_scan(G[:, T - 1::-1], onesb, tA[:, ::-1], 0.0, ALU.mult, ALU.add), [tA, self.onef], [G])
                    hi = G[:, 0:T].rearrange("p (c j) -> p c j", j=128)
                    lo = G[:, 128:T + 1:128].unsqueeze(2).to_broadcast([128, NCH, 128])
                    tot = tA[:, 0:T:128]
                tA3 = tA[:, :].rearrange("p (c j) -> p c j", j=128)
                tB3 = tB[:, :].rearrange("p (c j) -> p c j", j=128)
                self.tt(tA, tA3, G, hi, G, lo, ALU.subtract)
                self.act(tB, tB[:, :], tA, tA[:, :], AF.Exp, scale=-1.0 / 16)
                self.tt(qe, qe[:, :], qT, qT[:, :], tB, tB[:, :], ALU.mult)
                self.act(tB, tB[:, :], tA, tA[:, :], AF.Exp, scale=1.0 / 16)
                self.tt(ke, ke[:, :], kT, kT[:, :], tB, tB[:, :], ALU.mult)
                self.tt(tB, tB3, tA, tA3, tA, tot.unsqueeze(2).to_broadcast([128, NCH, 128]), ALU.subtract)
                self.act(tB, tB[:, :], tB, tB[:, :], AF.Exp, scale=1.0 / 16)
                self.tt(kd, kd[:, :], kT, kT[:, :], tB, tB[:, :], ALU.mult)
                self.act(ebt, ebt[:, :], tA, tot, AF.Exp, scale=-1.0 / 16)
                P.op("dve", lambda e: e.memset(Sf[:, :], 0.0), [], [Sf])
                P.op("dve", lambda e: e.memset(Sb[:, :], 0.0), [], [Sb])
                order = list(range(NCH)) if d == 0 else [1, 0] + list(range(NCH - 1, 1, -1))
                msk = self.UF if d == 0 else self.UB
                for c in order:
                    cs = slice(c * 128, (c + 1) * 128)
                    kt = kdt.next(); am = atm.next()
                    self.tr(self.pT, [(self.pT[:, 0:128], kd[:, cs])], [kd, self.ident], self.ident[:, :])
                    self.cp(kt, kt[:, :], self.pT, self.pT[:, 0:128], eng="act")
                    pa = self.pmm.next()
                    self.mm(pa, [(pa[:, 0:128], [(ke[:, cs], qe[:, cs])])], [ke, qe])
                    self.tt(am, am[:, :], pa, pa[:, 0:128], self.mk, msk, ALU.mult)
                    po = self.pmm.next()
                    self.mm(po, [(po[:, j * 128:(j + 1) * 128], [(vtok[:, c, j * 128:(j + 1) * 128], am[:, :]),
                                                                  (Sb[:, j * 128:(j + 1) * 128], qe[:, cs])]) for j in range(2)],
                            [vtok, am, Sb, qe])
                    po3 = po[:, 0:256].rearrange("p (a b) -> p a b", b=128)
                    if d == 0:
                        self.cp(ob, obh[:, :, cs], po, po3, eng="act")
                    else:
                        self.tt(ob, obh[:, :, cs], po, po3, ob, obh[:, :, cs], ALU.add)
                    pS = self.pmm.next()
                    self.mm(pS, [(pS[:, 0:256], [(kt[:, :], vtok[:, c, :])])], [kt, vtok])
                    self.stt(Sf, Sf[:, :], Sf, Sf[:, :], ebt[:, c:c + 1], pS, pS[:, 0:256], ALU.mult, ALU.add, rd=[ebt])
                    self.cp(Sb, Sb[:, :], Sf, Sf[:, :], eng="act")
            for (n0, nsz) in NTILES:
                pss = self.pmm.next()
                sqs = []
                for j in range(2):
                    sq = self.bring.next()
                    self.tt(sq, sq[:, 0:nsz], ob, obh[:, j, n0:n0 + nsz], ob, obh[:, j, n0:n0 + nsz], ALU.mult)
                    sqs.append(sq)
                self.mm(pss, [(pss[:, 0:nsz], [(self.ones[:, :], sq[:, 0:nsz]) for sq in sqs])], sqs + [self.ones])
                self.ts(tA, tA[:, n0:n0 + nsz], pss, pss[:, 0:nsz], 1.0 / 256, EPS, ALU.mult, ALU.add)
            self.act(tA, tA[:, :], tA, tA[:, :], AF.Sqrt)
            P.op("dve", lambda e: e.reciprocal(tA[:, :], tA[:, :]), [tA], [tA])
            for j in range(2):
                def cons(ps, n0, nsz, j=j):
                    g = self.bring.next()
                    self.act(g, g[:, 0:nsz], ps, ps[:, 0:nsz], AF.Silu)
                    oa = obh[:, j, n0:n0 + nsz]
                    self.stt(ob, oa, ob, oa, self.ppc(l, "gng", j), tA, tA[:, n0:n0 + nsz], ALU.mult, ALU.mult, rd=[self.pp])
                    self.tt(ob, oa, ob, oa, g, g[:, 0:nsz], ALU.mult)
                self.win(l, BIDX["gg"] + h * 2 + j, cons)
        self.project(l, ob, ob, 8, self.wpb_d, 1, mode, tiles)

    def ssd(self, l, b, tiles):
        P, A = self.P, self.A
        A.reset()
        ll = 32 if l % 2 else 64
        bo = A.view("bo", [128, 4, T], BF16)
        xtok = A.view("xtok", [128, NCH, 512], BF16)
        BT = A.view("BT", [128, T], BF16); CTt = A.view("CT", [128, T], BF16); Btok = A.view("Btok", [128, NCH, 128], BF16)
        dtk = A.view("dtk", [128, NCH, 64], F32); adk = A.view("adk", [128, NCH, 64], F32)
        adb = A.view("adb", [128, NCH, 64], BF16); ddt = A.view("ddt", [128, NCH, 64], F32)
        dtot = A.view("dtot", [128, NCH, 64], F32)
        xsf = Ring([A.view("xsf%d" % i, [128, T], BF16) for i in range(1)])
        ctmp = Ring([A.view("ctmp%d" % i, [128, 512], F32) for i in range(2)])
        import os
        PE2 = os.environ.get("PE2", "pool")
        Rs = [A.view("R%d" % i, [128, 8, 128], BF16) for i in range(2)]
        LTs = [A.view("LT%d" % i, [128, 8, 128], BF16) for i in range(2)]
        ECs = [A.view("EC%d" % i, [128, 8, 128], BF16) for i in range(2)]
        cbs = [A.view("cbm%d" % i, [128, 128], BF16) for i in range(2)]
        xdts = [A.view("xdt%d" % i, [128, 8, 64], BF16) for i in range(2)]
        xdds = [A.view("xdd%d" % i, [128, 8, 64], BF16) for i in range(2)]
        xdr = Ring(xdts + xdds)
        Sfs = [A.view("Sf%d" % i, [128, 8, 64], F32) for i in range(2)]
        Sbs = [A.view("Sbf%d" % i, [128, 512], BF16) for i in range(2)]
        rs = A.view("rs", [128, 512], F32)
        sdb = A.view("sdbc", [128, 128], F32)
        P.dma("sp", sdb, sdb[:, :], self.rows_d, self.rows_d[l, 1:2, 1024:1152].partition_broadcast(128))
        self.act(sdb, sdb[:, 64:128], sdb, sdb[:, 64:128], AF.Exp)
        self.ts(sdb, sdb[:, 64:128], sdb, sdb[:, 64:128], -1.0, None, ALU.mult)
        wdt = self.wring.next()
        P.dma("pool", wdt, wdt[:, :, :], self.win_d, self.win_d[l, BIDX["dt"]])
        for c0 in range(0, NCH, 8):
            cn = min(8, NCH - c0)
            ps = self.pmm.next()
            self.mm(ps, [(ps[:, i * 64:(i + 1) * 64], [(self.hT[:, k, (c0 + i) * 128:(c0 + i + 1) * 128], wdt[:, k, 0:64]) for k in range(8)])
                         for i in range(cn)], [self.hT, wdt])
            da = dtk[:, c0:c0 + cn, :]
            self.tt(dtk, da, ps, ps[:, 0:cn * 64].rearrange("p (a b) -> p a b", b=64), sdb,
                    sdb[:, 0:64].unsqueeze(1).to_broadcast([128, cn, 64]), ALU.add)
            self.act(dtk, da, dtk, da, AF.Exp)
            self.act(dtk, da, dtk, da, AF.Ln, bias=1.0)
        self.ck("dt0")
        self.tt(adk, adk[:, :, :], dtk, dtk[:, :, :], sdb, sdb[:, 64:128].unsqueeze(1).to_broadcast([128, NCH, 64]), ALU.mult)
        self.ck("dt1")
        self.cp(adb, adb[:, :, :], adk, adk[:, :, :])
        for c0 in range(0, NCH, 8):
            cn = min(8, NCH - c0)
            ps = self.pmm.next()
            self.mm(ps, [(ps[:, i * 64 + d * 32: i * 64 + d * 32 + 32], [((self.TL if d == 0 else self.TU), adb[:, c0 + i, d * 32:(d + 1) * 32])])
                         for i in range(cn) for d in range(2)], [self.mk, adb])
            self.act(ddt, ddt[:, c0:c0 + cn, :], ps, ps[:, 0:cn * 64].rearrange("p (a b) -> p a b", b=64), AF.Exp)
            ps2 = self.pmm.next()
            self.mm(ps2, [(ps2[:, i * 64:(i + 1) * 64], [(self.ones[:, :], adb[:, c0 + i, :])]) for i in range(cn)], [self.ones, adb])
            self.act(dtot, dtot[:, c0:c0 + cn, :], ps2, ps2[:, 0:cn * 64].rearrange("p (a b) -> p a b", b=64), AF.Exp)
        self.tt(ddt, ddt[:, :, :], ddt, ddt[:, :, :], dtk, dtk[:, :, :], ALU.mult)
        self.ck("dt2")

        for g in range(4):
            for (dst, bi) in ((BT, 16 + g), (CTt, 20 + g)):
                def cbc(ps, n0, nsz, dst=dst, bi=bi):
                    ct = ctmp.next()
                    self.conv(l, ps, n0, nsz, ct, ct[:, 0:nsz], "ccw", "ccb", bi, ll)
                    self.act(dst, dst[:, n0:n0 + nsz], ct, ct[:, 0:nsz], AF.Silu)
                self.win(l, BIDX["xbc"] + bi, cbc)
            for c0 in range(0, NCH, 8):
                cn = min(8, NCH - c0)
                self.tr(self.pT, [(self.pT[:, i * 128:(i + 1) * 128], BT[:, (c0 + i) * 128:(c0 + i + 1) * 128]) for i in range(cn)],
                        [BT, self.ident], self.ident[:, :])
                self.cp(Btok, Btok[:, c0:c0 + cn, :], self.pT, self.pT[:, 0:cn * 128].rearrange("p (a b) -> p a b", b=128), eng="act")
            self.ck("bc")
            for i in range(4):
                bi = g * 4 + i
                xs = xsf.next()

                def cxs(ps, n0, nsz, xs=xs, bi=bi):
                    ct = ctmp.next()
                    self.conv(l, ps, n0, nsz, ct, ct[:, 0:nsz], "ccw", "ccb", bi, ll)
                    self.act(xs, xs[:, n0:n0 + nsz], ct, ct[:, 0:nsz], AF.Silu)
                self.win(l, BIDX["xbc"] + bi, cxs)
                self.ts(bo, bo[:, i, :], xs, xs[:, :], self.ppc(l, "sdd", bi), None, ALU.mult, rd=[self.pp])
                for c0 in range(0, NCH, 8):
                    cn = min(8, NCH - c0)
                    self.tr(self.pT, [(self.pT[:, k * 128:(k + 1) * 128], xs[:, (c0 + k) * 128:(c0 + k + 1) * 128]) for k in range(cn)],
                            [xs, self.ident], self.ident[:, :])
                    self.cp(xtok, xtok[:, c0:c0 + cn, i * 128:(i + 1) * 128], self.pT,
                            self.pT[:, 0:cn * 128].rearrange("p (a b) -> p a b", b=128), eng=("act" if (c0 // 8) % 2 else "dve"))
            self.ck("xs")
            orders = [list(range(NCH)), [1, 0] + list(range(NCH - 1, 1, -1))]
            pbufs = [(t_, t_.h[:, :]) for t_ in self.pmm.t] + [(self.pT, self.pT.h[:, :].bitcast(F32))]
            for d in range(2):
                Sf = Sfs[0]; Sb = Sbs[0]
                P.op("dve", lambda e: e.memset(Sfs[0][:, :, :], 0.0), [], [Sf])
                P.op("dve", lambda e: e.memset(Sbs[0][:, :], 0.0), [], [Sb])
                U = self.UF if d == 0 else self.UB
                W = self.TL if d == 0 else self.TU
                hc = slice(d * 32 + g * 8, d * 32 + g * 8 + 8)

                def stageA(idx):
                    c = orders[d][idx]; k = idx % 2
                    cs = slice(c * 128, (c + 1) * 128)
                    R = Rs[k]; LT = LTs[k]; EC = ECs[k]; cbm = cbs[k]; xdt = xdts[k]; xdd = xdds[k]
                    pcy_t, pcy = pbufs[2 + k]
                    pst_t, pst = pbufs[k]
                    x3 = xtok[:, c, :].rearrange("p (a b) -> p a b", b=64)
                    self.tt(xdd, xdd[:, :, :], xtok, x3, ddt, ddt[:, c, hc].unsqueeze(2).to_broadcast([128, 8, 64]), ALU.mult, eng=PE2)
                    self.tt(xdt, xdt[:, :, :], xtok, x3, dtk, dtk[:, c, hc].unsqueeze(2).to_broadcast([128, 8, 64]), ALU.mult, eng=PE2)
                    self.tt(R, R[:, :, :], adk, adk[:, c, hc].unsqueeze(2).to_broadcast([128, 8, 128]),
                            self.mk, U.unsqueeze(1).to_broadcast([128, 8, 128]), ALU.mult)
                    self.mm(pcy_t, [(pcy[:, 0:128], [(BT[:, cs], CTt[:, cs])])], [BT, CTt])
                    Rf = R[:, :, :].rearrange("p a b -> p (a b)")
                    self.mm(self.pA, [(self.pA[:, hh * 512:(hh + 1) * 512], [(W, Rf[:, hh * 512:(hh + 1) * 512])]) for hh in range(2)],
                            [self.mk, R])
                    self.mm(self.pB, [(self.pB[:, hh * 512:(hh + 1) * 512], [(self.ones[:, :], Rf[:, hh * 512:(hh + 1) * 512])]) for hh in range(2)],
                            [self.ones, R])
                    self.mm(pst_t, [(pst[:, :], [(Btok[:, c, :], xdd[:, :, :].rearrange("p a b -> p (a b)"))])], [Btok, xdd])
                    self.tt(cbm, cbm[:, :], pcy_t, pcy[:, 0:128], self.mk, U, ALU.mult)
                    self.act(LT, LT[:, :, :], self.pA, self.pA[:, :].rearrange("p (a b) -> p a b", b=128), AF.Exp)
                    self.act(EC, EC[:, :, :], self.pB, self.pB[:, :].rearrange("p (a b) -> p a b", b=128), AF.Exp)
                    self.tt(LT, LT[:, :, :], LT, LT[:, :, :], cbm, cbm[:, :].unsqueeze(1).to_broadcast([128, 8, 128]), ALU.mult)
                    self.tt(EC, EC[:, :, :], EC, EC[:, :, :], CTt, CTt[:, cs].unsqueeze(1).to_broadcast([128, 8, 128]), ALU.mult, eng=PE2)

                def stageB(idx):
                    c = orders[d][idx]; k = idx % 2
                    cs = slice(c * 128, (c + 1) * 128)
                    LT = LTs[k]; EC = ECs[k]; xdt = xdts[k]
                    pcy_t, pcy = pbufs[2 + k]
                    pst_t, pst = pbufs[k]
                    groups = []
                    for hp in range(4):
                        for e_ in range(2):
                            hh = 2 * hp + e_
                            groups.append((pcy[e_ * 64:(e_ + 1) * 64, hp * 128:(hp + 1) * 128],
                                           [(xdt[:, hh, :], LT[:, hh, :]), (Sb[:, hh * 64:(hh + 1) * 64], EC[:, hh, :])]))
                    self.mm(pcy_t, groups, [xdt, LT, Sb, EC])
                    self.tt(Sf, Sf[:, :, :], Sf, Sf[:, :, :], dtot, dtot[:, c, hc].unsqueeze(2).to_broadcast([128, 8, 64]), ALU.mult)
                    self.tt(Sf, Sf[:, :, :], Sf, Sf[:, :, :], pst_t, pst[:, :].rearrange("p (a b) -> p a b", b=64), ALU.add)
                    self.cp(Sb, Sb[:, :], Sf, Sf[:, :, :].rearrange("p a b -> p (a b)"), eng="act")
                    self.tt(bo, bo[:, :, cs], pcy_t, pcy[:, :].rearrange("p (a b) -> p a b", b=128), bo, bo[:, :, cs], ALU.add)

                stageA(0)
                for idx in range(NCH):
                    if idx + 1 < NCH:
                        stageA(idx + 1)
                    stageB(idx)
            self.ck("scan")
            for i in range(4):
                def cz(ps, n0, nsz, i=i):
                    gt = self.bring.next()
                    self.act(gt, gt[:, 0:nsz], ps, ps[:, 0:nsz], AF.Silu)
                    self.tt(bo, bo[:, i, n0:n0 + nsz], bo, bo[:, i, n0:n0 + nsz], gt, gt[:, 0:nsz], ALU.mult)
                self.win(l, BIDX["z"] + g * 4 + i, cz)
            for (n0, nsz) in NTILES:
                pss = self.pmm.next()
                sqs = []
                for i in range(4):
                    sq = xdr.next()
                    sqa = sq[:, :, :].rearrange("p a b -> p (a b)")[:, 0:nsz]
                    self.tt(sq, sqa, bo, bo[:, i, n0:n0 + nsz], bo, bo[:, i, n0:n0 + nsz], ALU.mult)
                    sqs.append((sq, sqa))
                self.mm(pss, [(pss[:, 0:nsz], [(self.ones[:, :], a_) for (_, a_) in sqs])], [s_ for (s_, _) in sqs] + [self.ones])
                self.ts(rs, rs[:, 0:nsz], pss, pss[:, 0:nsz], 1.0 / 512, EPS, ALU.mult, ALU.add)
                self.act(rs, rs[:, 0:nsz], rs, rs[:, 0:nsz], AF.Sqrt)
                P.op("dve", lambda e, nsz=nsz: e.reciprocal(rs[:, 0:nsz], rs[:, 0:nsz]), [rs], [rs])
                for i in range(4):
                    oa = bo[:, i, n0:n0 + nsz]
                    self.stt(bo, oa, bo, oa, self.ppc(l, "sng", g * 4 + i), rs, rs[:, 0:nsz], ALU.mult, ALU.mult, rd=[self.pp])
            self.ck("norm")
            for m in range(8):
                wt = self.pring.next()
                P.dma("pool", wt, wt[:, 0:4, :], self.wpc_d, self.wpc_d[l, m, :, g * 4:(g + 1) * 4, :])
                for (n0, nsz) in tiles:
                    pp_ = self.pmm.next()
                    self.mm(pp_, [(pp_[:, 0:nsz], [(wt[:, k, :], bo[:, k, n0:n0 + nsz]) for k in range(4)])], [wt, bo])
                    ma = self.merged[:, m, n0:n0 + nsz]
                    if g == 0:
                        self.cp(self.merged, ma, pp_, pp_[:, 0:nsz], eng="act")
                    else:
                        self.tt(self.merged, ma, pp_, pp_[:, 0:nsz], self.merged, ma, ALU.add)
        for m in range(8):
            def cg(ps, n0, nsz, m=m):
                gt = self.bring.next()
                self.act(gt, gt[:, 0:nsz], ps, ps[:, 0:nsz], AF.Sigmoid)
                ma = self.merged[:, m, n0:n0 + nsz]
                self.tt(self.merged, ma, self.merged, ma, gt, gt[:, 0:nsz], ALU.mult)
            self.win(l, BIDX["mg"] + 16 + m, cg, tiles)

    def final(self, l, b, skip_ctx):
        P, A = self.P, self.A
        A.reset()
        wo = A.view("wo", [128, 8, D], BF16)
        P.dma("pool", wo, wo[:, :, :], self.wout_d, self.wout_d[l])
        Gb = A.view("Gb", [128, D], F32); Gc = A.view("Gc", [128, D], F32)
        self.bload(Gb, l, b, 2); self.bload(Gc, l, 2, 2)
        xr = Ring([A.view("xf%d" % i, [128, D], F32) for i in range(3)])
        tr_ = Ring([A.view("tf%d" % i, [128, D], F32) for i in range(2)])
        junk = A.view("junkf", [128, D], BF16)
        for j in range(2 if skip_ctx else 0, NCH):
            xt = xr.next(); t = tr_.next(); st = self.stat.next()
            self.xdma(l, b, j, xt, xt[:, :], True, None)
            cs = slice(j * 128, (j + 1) * 128)
            self.mm(self.pA, [(self.pA[:, hh * 512:(hh + 1) * 512], [(self.merged[:, k, cs], wo[:, k, hh * 512:(hh + 1) * 512]) for k in range(8)])
                              for hh in range(2)], [self.merged, wo])
            P.op("dve", lambda e, st=st: e.memset(st[:, 0:1], 0.0), [], [st])
            self.act(junk, junk[:, :], self.pA, self.pA[:, :], AF.Square, accum=st[:, 0:1], wr=[st])
            self.rstd(st, 0, 1.0 / D)
            Ga = Gc if j < 2 else Gb
            self.stt(t, t[:, :], self.pA, self.pA[:, :], st[:, 0:1], Ga, Ga[:, :], ALU.mult, ALU.mult, rd=[st])
            self.tt(t, t[:, :], t, t[:, :], xt, xt[:, :], ALU.add)
            self.xdma(l, b, j, t, t[:, :], False, None)

    def build(self, stop=99, nb=2):
        try:
            self.build_(stop, nb)
        except StopBuild:
            pass
        P = self.P
        self.A.reset()
        for eng in ("sp",):
            w = P._waits(eng, [self.out_d], [self.out_d])
            P.ops[eng].append((w, None, None, 0))
        P.emit()
        return self.nc

    def build_(self, stop=99, nb=2):
        P = self.P
        self.setup()
        if stop >= 1:
            self.adaln()
        for l in range(self.nl):
            last = (l == self.nl - 1)
            tiles = NTILES[1:] if last else NTILES
            for b in range(nb):
                if stop >= 2:
                    self.phaseA(l, b)
                if stop >= 3:
                    self.ssd(l, b, tiles)
                if stop >= 4:
                    self.gla(l, b, tiles, "add")
                if stop >= 5:
                    self.lru(l, b, tiles, "add")
                if stop >= 6:
                    self.final(l, b, last)


def prep_shared(inp):
    f = lambda a: np.ascontiguousarray(np.asarray(a, dtype=np.float32))
    w_in = f(inp["w_in"])
    win = np.zeros((L, NBLK, 128, 8, 128), np.float32)
    for i, (c0, m) in enumerate(BLKS):
        win[:, i, :, :, :m] = w_in[:, :, c0:c0 + m].reshape(L, 8, 128, m).transpose(0, 2, 1, 3)
    sh = {"w_in": win}
    sh["ada_w"] = f(f(inp["ada_w"]).reshape(L, 8, 128, 3072).transpose(0, 2, 1, 3))
    rows = np.zeros((L, 3, 3072), np.float32)
    rows[:, 0, :] = f(inp["ada_b"]); rows[:, 1, :1024] = f(inp["pre_g"]); rows[:, 2, :1024] = f(inp["post_g"])
    rows[:, 1, 1024:1088] = f(inp["ssd_dt_bias"]).reshape(L, 64); rows[:, 1, 1088:1152] = f(inp["ssd_a_log"]).reshape(L, 64)
    sh["rows"] = rows
    pp = np.zeros((128, L, PPW), np.float32)

    def put(name, arr):
        n = arr.shape[1]
        pp[:, :, PPL[name]:PPL[name] + n] = arr.transpose(2, 0, 1)
    put("caw", f(inp["conv_a_w"]).reshape(L, 4, 8, 128).transpose(0, 2, 1, 3).reshape(L, 32, 128))
    put("cab", f(inp["conv_a_b"]).reshape(L, 8, 128))
    put("lbr", f(inp["lru_br"]).reshape(L, 16, 128))
    put("lbi", f(inp["lru_bi"]).reshape(L, 16, 128))
    put("llam", f(inp["lru_lam"]).reshape(L, 16, 128))
    put("gab", f(inp["gla_alpha_b"]).reshape(L, 8, 128))
    put("gng", f(inp["gla_norm_g"]).reshape(L, 2, 128))
    put("ccw", f(inp["conv_c_w"]).reshape(L, 4, 24, 128).transpose(0, 2, 1, 3).reshape(L, 96, 128))
    put("ccb", f(inp["conv_c_b"]).reshape(L, 24, 128))
    put("sdd", np.repeat(f(inp["ssd_d"]).reshape(L, 16, 2), 64, axis=2))
    put("sng", f(inp["ssd_norm_g"]).reshape(L, 16, 128))
    z64 = np.zeros((L, 1, 64), np.float32)
    put("sal", np.concatenate([f(inp["ssd_a_log"]).reshape(L, 1, 64), z64], 2))
    put("sdb", np.concatenate([f(inp["ssd_dt_bias"]).reshape(L, 1, 64), z64], 2))
    sh["pp"] = np.ascontiguousarray(pp.reshape(128, NPP))
    wr = f(inp["lru_wr"]); wi = f(inp["lru_wi"])
    lw = np.stack([wr, wi], 1)
    sh["lruw"] = np.ascontiguousarray(lw.transpose(0, 3, 4, 1, 2, 5).reshape(L, 8, 128, 4, 128))
    sh["gup"] = np.ascontiguousarray(f(inp["gla_alpha_up"]).transpose(0, 2, 1, 3))
    for nm, key, nk in (("wpa", "w_pa", 8), ("wpb", "w_pb", 8), ("wpc", "w_pc", 16)):
        w = f(inp[key]).reshape(L, nk, 128, 8, 128)
        sh[nm] = np.ascontiguousarray(w.transpose(0, 3, 2, 1, 4))
    sh["wout"] = np.ascontiguousarray(f(inp["w_out"]).reshape(L, 8, 128, 1024).transpose(0, 2, 1, 3))
    return sh


def core_inputs(inp, sh, i):
    f = lambda a: np.ascontiguousarray(np.asarray(a, dtype=np.float32))
    m = dict(sh)
    m["x"] = f(inp["x"][2 * i:2 * i + 2])
    m["ctx"] = f(inp["ctx"][2 * i:2 * i + 2])
    crow = np.stack([f(inp["c"][2 * i]), f(inp["c"][2 * i + 1]), f(inp["c_ctx"])], 0)
    m["cT"] = np.ascontiguousarray(crow.reshape(3, 8, 128).transpose(2, 1, 0))
    return m


_NC = {}


def kernel(**inputs):
    if "nc" not in _NC:
        _NC["nc"] = K(L).build()
    sh = prep_shared(inputs)
    in_maps = [core_inputs(inputs, sh, i) for i in range(8)]
    res = run_bass_kernel_spmd(_NC["nc"], in_maps, core_ids=list(range(8)))
    return np.concatenate([r["out"] for r in res.results], axis=0).astype(np.float32)
```

```python
from contextlib import ExitStack
import numpy as np
import concourse.bass as bass
import concourse.mybir as mybir
from concourse.bass_utils import run_bass_kernel_spmd

F32 = mybir.dt.float32
BF16 = mybir.dt.bfloat16
AF = mybir.ActivationFunctionType
ALU = mybir.AluOpType

L = 4
D = 1024
S = 2048
CT = 256
T = CT + S
NCH = T // 128
EPS = 1e-6
NTILES = [(0, 256), (256, 512), (768, 512), (1280, 512), (1792, 512)]
O_LX, O_LG, O_Q, O_K, O_V, O_GG, O_LOW, O_Z, O_XBC, O_DT, O_MG = (
    0, 1024, 2048, 2560, 3072, 4096, 5120, 5152, 7200, 10272, 10336)
BLKS = []
BIDX = {}


def _mkblks():
    def add(name, c0, n, m=128):
        BIDX[name] = len(BLKS)
        for i in range(n):
            BLKS.append((c0 + i * m, m))
    add("lx", O_LX, 8); add("lg", O_LG, 8); add("q", O_Q, 4); add("k", O_K, 4)
    add("v", O_V, 8); add("gg", O_GG, 8); add("low", O_LOW, 2, 16); add("z", O_Z, 16)
    add("xbc", O_XBC, 24); add("dt", O_DT, 1, 64); add("mg", O_MG, 24)


_mkblks()
NBLK = len(BLKS)

PPL = {}
_o = 0
for _n, _w in (("caw", 32), ("cab", 8), ("lbr", 16), ("lbi", 16), ("llam", 16), ("gab", 8), ("gng", 2),
               ("ccw", 96), ("ccb", 24), ("sdd", 16), ("sng", 16), ("sal", 1), ("sdb", 1)):
    PPL[_n] = _o
    _o += _w
PPW = _o
NPP = PPW * L


class Tk:
    def __init__(self, name, h, semkey=None):
        self.name = name
        self.h = h
        self.last_w = None
        self.readers = []
        self.semkey = semkey or ("D_" + name)

    def __getitem__(self, idx):
        return self.h[idx]


class Prog:
    ENG = ("pe", "act", "dve", "pool", "sp")

    def __init__(self, nc):
        self.nc = nc
        self.es = ExitStack()
        self.ops = {e: [] for e in self.ENG}
        self.cnt = {e: 0 for e in self.ENG}
        self.waited = {e: {} for e in self.ENG}
        self.sems = {}
        self.dcount = {}
        for e in self.ENG:
            self._sem("E_" + e)

    def _sem(self, key):
        if key not in self.sems:
            self.sems[key] = self.es.enter_context(self.nc.semaphore("s%d" % len(self.sems)))
        return self.sems[key]

    def sbuf(self, name, shape, dtype):
        return Tk(name, self.es.enter_context(self.nc.sbuf_tensor("sb_" + name, list(shape), dtype)))

    def psum(self, name, shape, dtype=F32):
        return Tk(name, self.es.enter_context(self.nc.psum_tensor("ps_" + name, list(shape), dtype)))

    def dram(self, name, shape, dtype, kind="Internal"):
        return Tk(name, self.nc.dram_tensor(name, list(shape), dtype, kind=kind))

    def _waits(self, eng, reads, writes):
        evs = []
        for t in reads:
            if t.last_w is not None:
                evs.append(t.last_w)
        for t in writes:
            if t.last_w is not None:
                evs.append(t.last_w)
            evs.extend(t.readers)
        w = {}
        wd = self.waited[eng]
        for (k, v) in evs:
            if wd.get(k, 0) >= v:
                continue
            if w.get(k, 0) < v:
                w[k] = v
        for k, v in w.items():
            wd[k] = v
        return list(w.items())

    def _commit(self, ev, reads, writes):
        for t in reads:
            t.readers.append(ev)
        for t in writes:
            t.last_w = ev
            t.readers = []

    def op(self, eng, fn, reads=(), writes=()):
        waits = self._waits(eng, reads, writes)
        self.cnt[eng] += 1
        ev = ("E_" + eng, self.cnt[eng])
        self.ops[eng].append((waits, fn, ev[0], 1))
        self._commit(ev, reads, writes)

    def dma(self, q, out_t, out_ap, in_t, in_ap):
        key = out_t.semkey
        self._sem(key)
        waits = self._waits(q, [in_t], [out_t])
        self.dcount[key] = self.dcount.get(key, 0) + 16
        ev = (key, self.dcount[key])

        def fn(e, out_ap=out_ap, in_ap=in_ap):
            return e.dma_start(out=out_ap, in_=in_ap)
        self.ops[q].append((waits, fn, key, 16))
        self._commit(ev, [in_t], [out_t])

    def barrier(self, tiles):
        for eng in self.ENG:
            waits = self._waits(eng, [], tiles)
            if waits:
                self.ops[eng].append((waits, None, None, 0))

    def emit(self):
        nc = self.nc
        sems = self.sems
        ops = self.ops

        def run(e, lst):
            for (waits, fn, sk, inc) in lst:
                for (k, v) in waits:
                    e.wait_ge(sems[k], v)
                if fn is not None:
                    fn(e).then_inc(sems[sk], inc)

        with nc.Block() as block:
            @block.tensor
            def _(e):
                run(e, ops["pe"])

            @block.scalar
            def _(e):
                run(e, ops["act"])

            @block.vector
            def _(e):
                run(e, ops["dve"])

            @block.gpsimd
            def _(e):
                run(e, ops["pool"])

            @block.sync
            def _(e):
                run(e, ops["sp"])
        self.es.close()


class Ring:
    def __init__(self, tiles):
        self.t = tiles
        self.i = 0

    def next(self):
        t = self.t[self.i % len(self.t)]
        self.i += 1
        return t


class Arena:
    def __init__(self, P, nbytes):
        self.P = P
        self.t = P.sbuf("arena", [128, nbytes // 4], F32)
        self.nbytes = nbytes
        self.off = 0
        self.live = []
        self.gen = 0

    def reset(self):
        self.P.barrier(self.live)
        self.live = []
        self.off = 0
        self.gen += 1

    def view(self, name, shape, dtype, semkey=None):
        esz = 4 if dtype == F32 else 2
        n = 1
        for s in shape[1:]:
            n *= s
        nb = (n * esz + 31) // 32 * 32
        assert self.off + nb <= self.nbytes, (name, self.off, nb, self.nbytes)
        a = self.t.h[0:shape[0], self.off // 4:(self.off + nb) // 4]
        if dtype != F32:
            a = a.bitcast(dtype)
        a = a[:, 0:n]
        if len(shape) == 3:
            a = a.rearrange("p (a b) -> p a b", b=shape[2])
        elif len(shape) == 4:
            a = a.rearrange("p (a b c) -> p a b c", b=shape[2], c=shape[3])
        self.off += nb
        t = Tk("%s_g%d" % (name, self.gen), a, semkey=semkey or ("DA_" + name))
        self.live.append(t)
        return t


class StopBuild(Exception):
    pass


class K:
    def ck(self, tag):
        import os
        if os.environ.get("CK") == tag:
            raise StopBuild(tag)

    def __init__(self, nl=L):
        self.nl = nl
        nc = bass.Bass("TRN2", target_bir_lowering=False)
        self.nc = nc
        self.P = P = Prog(nc)
        d = P.dram
        self.x_d = d("x", [2, S, D], F32, "ExternalInput")
        self.ctx_d = d("ctx", [2, CT, D], F32, "ExternalInput")
        self.cT_d = d("cT", [128, 8, 3], F32, "ExternalInput")
        self.adaw_d = d("ada_w", [L, 128, 8, 3072], F32, "ExternalInput")
        self.rows_d = d("rows", [L, 3, 3072], F32, "ExternalInput")
        self.win_d = d("w_in", [L, NBLK, 128, 8, 128], F32, "ExternalInput")
        self.pp_d = d("pp", [128, NPP], F32, "ExternalInput")
        self.lruw_d = d("lruw", [L, 8, 128, 4, 128], F32, "ExternalInput")
        self.gup_d = d("gup", [L, 16, 2, 512], F32, "ExternalInput")
        self.wpa_d = d("wpa", [L, 8, 128, 8, 128], F32, "ExternalInput")
        self.wpb_d = d("wpb", [L, 8, 128, 8, 128], F32, "ExternalInput")
        self.wpc_d = d("wpc", [L, 8, 128, 16, 128], F32, "ExternalInput")
        self.wout_d = d("wout", [L, 128, 8, 1024], F32, "ExternalInput")
        self.out_d = d("out", [2, S, D], F32, "ExternalOutput")
        self.ctxs_d = d("ctxs", [2, CT, D], F32)
        self.mods_d = d("mods", [L, 3, 3, 1024], F32)

        self.hT = P.sbuf("hT", [128, 8, T], BF16)
        self.merged = P.sbuf("merged", [128, 8, T], BF16)
        self.pp = P.sbuf("pp", [128, NPP], F32)
        self.pd = P.sbuf("pd", [128, L * 32], F32)
        self.identf = P.sbuf("identf", [128, 128], F32)
        self.ident = P.sbuf("ident", [128, 128], BF16)
        self.mf = P.sbuf("mf", [128, 4, 128], F32)
        self.mk = P.sbuf("mk", [128, 4, 128], BF16)
        self.ones = P.sbuf("ones", [128, 128], BF16)
        self.onef = P.sbuf("onef", [128, 1], F32)
        self.stat = Ring([P.sbuf("stat%d" % i, [128, 4], F32) for i in range(4)])
        self.wring = Ring([P.sbuf("wblk%d" % i, [128, 8, 128], BF16) for i in range(4)])
        self.pring = Ring([P.sbuf("pblk%d" % i, [128, 8, 128], BF16) for i in range(2)])
        self.lring = Ring([P.sbuf("lblk%d" % i, [128, 4, 128], BF16) for i in range(2)])
        self.gup = P.sbuf("gup", [16, 2, 512], BF16)
        self.bring = Ring([P.sbuf("btmp%d" % i, [128, 512], BF16) for i in range(3)])
        used = (2 * 8 * T * 2 + NPP * 4 + L * 32 * 4 + 512 + 256 + 2048 + 1024 + 256 + 4 + 4 * 16
                + 4 * 2048 + 2 * 2048 + 2 * 1024 + 2048 + 3 * 1024)
        self.A = Arena(P, (211900 - used) // 32 * 32)
        self.pmm = Ring([P.psum("pmm%d" % i, [128, 512]) for i in range(3)])
        self.pA = P.psum("pA", [128, 1024])
        self.pB = P.psum("pB", [128, 1024])
        self.pT = P.psum("pT", [128, 1024], BF16)

    def act(self, ot, oap, it, iap, func, bias=None, scale=None, accum=None, rd=(), wr=()):
        kw = {}
        if bias is not None:
            kw["bias"] = bias
        if scale is not None:
            kw["scale"] = scale
        if accum is not None:
            kw["accum_out"] = accum
        self.P.op("act", lambda e: e.activation(oap, iap, func, **kw), [it] + list(rd), [ot] + list(wr))

    def tt(self, ot, oap, at, aap, bt, bap, op, eng="dve"):
        self.P.op(eng, lambda e: e.tensor_tensor(oap, aap, bap, op), [at, bt], [ot])

    def ts(self, ot, oap, at, aap, s1, s2, op0, op1=None, rd=(), eng="dve"):
        if op1 is None:
            self.P.op(eng, lambda e: e.tensor_scalar(oap, aap, s1, None, op0), [at] + list(rd), [ot])
        else:
            self.P.op(eng, lambda e: e.tensor_scalar(oap, aap, s1, s2, op0, op1), [at] + list(rd), [ot])

    def stt(self, ot, oap, at, aap, sc, bt, bap, op0, op1, rd=(), eng="dve"):
        self.P.op(eng, lambda e: e.scalar_tensor_tensor(oap, aap, sc, bap, op0, op1), [at, bt] + list(rd), [ot])

    def cp(self, ot, oap, it, iap, eng="dve"):
        if eng == "act":
            self.P.op("act", lambda e: e.activation(oap, iap, AF.Copy), [it], [ot])
        else:
            self.P.op(eng, lambda e: e.tensor_copy(oap, iap), [it], [ot])

    def mm(self, ot, groups, reads):
        def fn(e):
            ins = None
            for (oap, pairs) in groups:
                n = len(pairs)
                for i, (l_, r_) in enumerate(pairs):
                    ins = e.matmul(oap, l_, r_, start=(i == 0), stop=(i == n - 1))
            return ins
        self.P.op("pe", fn, reads, [ot])

    def tr(self, ot, items, reads, ident):
        def fn(e):
            ins = None
            for (oap, iap) in items:
                ins = e.transpose(oap, iap, ident)
            return ins
        self.P.op("pe", fn, reads, [ot])

    def rstd(self, st, col, n_inv):
        a = st[:, col:col + 1]
        self.ts(st, a, st, a, n_inv, EPS, ALU.mult, ALU.add)
        self.act(st, a, st, a, AF.Sqrt)
        self.P.op("dve", lambda e: e.reciprocal(a, a), [st], [st])

    def ppc(self, l, name, i=0):
        c = l * PPW + PPL[name] + i
        return self.pp[:, c:c + 1]

    def win(self, l, bi, consume, tiles=NTILES):
        c0, M = BLKS[bi]
        wt = self.wring.next()
        self.P.dma("pool", wt, wt[:, :, :], self.win_d, self.win_d[l, bi])
        for (n0, nsz) in tiles:
            ps = self.pmm.next()
            self.mm(ps, [(ps[0:M, 0:nsz], [(wt[:, k, 0:M], self.hT[:, k, n0:n0 + nsz]) for k in range(8)])],
                    [wt, self.hT])
            consume(ps, n0, nsz)

    def conv(self, l, ps, n0, nsz, dst_t, dst_ap, wname, bname, blk, ll):
        if n0 == 0:
            ll = nsz
        w = [self.ppc(l, wname, blk * 4 + k) for k in range(4)]
        self.act(dst_t, dst_ap, ps, ps[:, 0:nsz], AF.Identity, bias=self.ppc(l, bname, blk), scale=w[1], rd=[self.pp])
        d3 = dst_ap.rearrange("p (a b) -> p a b", b=ll)
        p3 = ps[:, 0:nsz].rearrange("p (a b) -> p a b", b=ll)
        for (k, so, do_) in ((0, slice(0, ll - 1), slice(1, ll)), (2, slice(1, ll), slice(0, ll - 1)),
                             (3, slice(2, ll), slice(0, ll - 2))):
            self.stt(dst_t, d3[:, :, do_], ps, p3[:, :, so], w[k], dst_t, d3[:, :, do_], ALU.mult, ALU.add,
                     rd=[self.pp])

    def setup(self):
        P = self.P
        P.dma("sp", self.pp, self.pp[:, :], self.pp_d, self.pp_d[:, :])
        P.op("pool", lambda e: e.memset(self.identf[:, :], 0.0), [], [self.identf])
        P.op("pool", lambda e: e.affine_select(self.identf[:, :], self.identf[:, :], [[-1, 128]], ALU.not_equal,
                                               1.0, base=0, channel_multiplier=1), [self.identf], [self.identf])
        self.cp(self.ident, self.ident[:, :], self.identf, self.identf[:, :])
        P.op("pool", lambda e: e.memset(self.mf[:, :, :], 1.0), [], [self.mf])
        for i, (pat, cm, cmp_) in enumerate((([[-1, 128]], 1, ALU.is_gt), ([[1, 128]], -1, ALU.is_gt),
                                             ([[1, 128]], -1, ALU.is_ge), ([[-1, 128]], 1, ALU.is_ge))):
            P.op("pool", lambda e, i=i, pat=pat, cm=cm, cmp_=cmp_: e.affine_select(
                self.mf[:, i, :], self.mf[:, i, :], pat, cmp_, 0.0, base=0, channel_multiplier=cm),
                [self.mf], [self.mf])
        self.cp(self.mk, self.mk[:, :, :], self.mf, self.mf[:, :, :])
        P.op("dve", lambda e: e.memset(self.ones[:, :], 1.0), [], [self.ones])
        P.op("dve", lambda e: e.memset(self.onef[:, :], 1.0), [], [self.onef])
        for l in range(L):
            b = l * 32
            lam = self.pp[:, l * PPW + PPL["llam"]: l * PPW + PPL["llam"] + 16]
            o = self.pd[:, b:b + 16]
            self.act(self.pd, o, self.pp, lam, AF.Exp, scale=-1.0)
            self.act(self.pd, o, self.pd, o, AF.Ln, bias=1.0)
            self.ts(self.pd, o, self.pd, o, -8.0, None, ALU.mult)
            gb = self.pp[:, l * PPW + PPL["gab"]: l * PPW + PPL["gab"] + 8]
            self.ts(self.pd, self.pd[:, b + 16:b + 24], self.pp, gb, -1.0, None, ALU.mult)
            al = self.ppc(l, "sal")
            self.act(self.pd, self.pd[:, b + 24:b + 25], self.pp, al, AF.Exp)
            self.ts(self.pd, self.pd[:, b + 24:b + 25], self.pd, self.pd[:, b + 24:b + 25], -1.0, None, ALU.mult)
        self.TL, self.TU, self.UF, self.UB = (self.mk[:, i, :] for i in range(4))

    def adaln(self):
        P, A = self.P, self.A
        A.reset()
        scT = A.view("scT", [128, 8, 3], F32)
        P.dma("sp", scT, scT[:, :, :], self.cT_d, self.cT_d[:, :, :])
        self.act(scT, scT[:, :, :], scT, scT[:, :, :], AF.Silu)
        awr = Ring([A.view("adaw%d" % i, [128, 8, 512], F32) for i in range(2)])
        rows = [A.view("rows%d" % i, [3, 3072], F32) for i in range(3)]
        modr = A.view("modr", [3, 3072], F32)
        modA = A.view("modA", [3, 2048], F32)
        for l in range(self.nl):
            for i in range(3):
                P.dma("sp", rows[i], rows[i][:, :], self.rows_d, self.rows_d[l, i:i + 1, :].partition_broadcast(3))
            for n in range(6):
                aw = awr.next()
                P.dma("sp", aw, aw[:, :, :], self.adaw_d, self.adaw_d[l, :, :, n * 512:(n + 1) * 512])
                ps = self.pmm.next()
                self.mm(ps, [(ps[0:3, :], [(scT[:, k, :], aw[:, k, :]) for k in range(8)])], [scT, aw])
                self.tt(modr, modr[:, n * 512:(n + 1) * 512], ps, ps[0:3, :], rows[0], rows[0][:, n * 512:(n + 1) * 512],
                        ALU.add)
            self.stt(modA, modA[:, 0:1024], modr, modr[:, 1024:2048], 1.0, rows[1], rows[1][:, 0:1024], ALU.add, ALU.mult)
            self.tt(modA, modA[:, 1024:2048], modr, modr[:, 2048:3072], rows[2], rows[2][:, 0:1024], ALU.mult)
            P.dma("sp", self.mods_d, self.mods_d[l, :, 0, :], modA, modA[:, 0:1024])
            P.dma("sp", self.mods_d, self.mods_d[l, :, 1, :], modr, modr[:, 0:1024])
            P.dma("sp", self.mods_d, self.mods_d[l, :, 2, :], modA, modA[:, 1024:2048])

    def xdma(self, l, b, j, sb_t, sb_ap, load, first_layer_src):
        P = self.P
        if j < 2:
            dt_ = (self.ctx_d if (load and l == 0) else self.ctxs_d)
            aps = [(dt_[b, j * 128:(j + 1) * 128, :], sb_ap)]
        else:
            dt_ = (self.x_d if (load and l == 0) else self.out_d)
            jj = j - 2
            if l % 2 == 0:
                aps = [(dt_[b, jj * 128:(jj + 1) * 128, :], sb_ap)]
            else:
                v = dt_[b].rearrange("(r w) d -> w r d", w=64)
                aps = [(v[jj * 4 + wl], sb_ap[wl * 32:(wl + 1) * 32, :]) for wl in range(4)]
        for (da, sa) in aps:
            if load:
                P.dma("sp", sb_t, sa, dt_, da)
            else:
                P.dma("sp", dt_, da, sb_t, sa)

    def bload(self, t, l, row, which):
        self.P.dma("sp", t, t[:, :], self.mods_d, self.mods_d[l, row, which:which + 1, :].partition_broadcast(128))

    def phaseA(self, l, b):
        P, A = self.P, self.A
        A.reset()
        Ab = A.view("Ab", [128, D], F32); Sb = A.view("Sb", [128, D], F32)
        Ac = A.view("Ac", [128, D], F32); Sc = A.view("Sc", [128, D], F32)
        self.bload(Ab, l, b, 0); self.bload(Sb, l, b, 1); self.bload(Ac, l, 2, 0); self.bload(Sc, l, 2, 1)
        xr = Ring([A.view("xa%d" % i, [128, D], F32) for i in range(3)])
        t1r = Ring([A.view("t1_%d" % i, [128, D], F32) for i in range(2)])
        hbr = Ring([A.view("hb%d" % i, [128, D], BF16) for i in range(2)])
        junk = A.view("junk", [128, D], BF16)
        for j in range(NCH):
            xt = xr.next(); t1 = t1r.next(); hb = hbr.next(); st = self.stat.next()
            self.xdma(l, b, j, xt, xt[:, :], True, None)
            P.op("dve", lambda e, st=st: e.memset(st[:, 0:1], 0.0), [], [st])
            self.act(junk, junk[:, :], xt, xt[:, :], AF.Square, accum=st[:, 0:1], wr=[st])
            self.rstd(st, 0, 1.0 / D)
            Aa, Sa = (Ac, Sc) if j < 2 else (Ab, Sb)
            self.stt(t1, t1[:, :], xt, xt[:, :], st[:, 0:1], Aa, Aa[:, :], ALU.mult, ALU.mult, rd=[st])
            self.tt(hb, hb[:, :], t1, t1[:, :], Sa, Sa[:, :], ALU.add)
            self.tr(self.pT, [(self.pT[:, k * 128:(k + 1) * 128], hb[:, k * 128:(k + 1) * 128]) for k in range(8)],
                    [hb, self.ident], self.ident[:, :])
            self.cp(self.hT, self.hT[:, :, j * 128:(j + 1) * 128],
                    self.pT, self.pT[:, :].rearrange("p (a b) -> p a b", b=128), eng=("act" if j % 2 else "dve"))

    def project(self, l, bo_t, bo_ap, nk, w_d, gi, mode, tiles):
        P = self.P
        for m in range(8):
            wts = []
            for k0 in range(0, nk, 8):
                wt = self.pring.next()
                P.dma("pool", wt, wt[:, :, :], w_d, w_d[l, m, :, k0:k0 + 8, :])
                wts.append(wt)
            gt = self.wring.next()
            bi = BIDX["mg"] + gi * 8 + m
            P.dma("pool", gt, gt[:, :, :], self.win_d, self.win_d[l, bi])
            for (n0, nsz) in tiles:
                pg = self.pmm.next()
                self.mm(pg, [(pg[:, 0:nsz], [(gt[:, k, :], self.hT[:, k, n0:n0 + nsz]) for k in range(8)])], [gt, self.hT])
                g = self.bring.next()
                self.act(g, g[:, 0:nsz], pg, pg[:, 0:nsz], AF.Sigmoid)
                pp_ = self.pmm.next()
                self.mm(pp_, [(pp_[:, 0:nsz], [(wts[k // 8][:, k % 8, :], bo_ap[:, k, n0:n0 + nsz]) for k in range(nk)])],
                        wts + [bo_t])
                ma = self.merged[:, m, n0:n0 + nsz]
                if mode == "set":
                    self.tt(self.merged, ma, pp_, pp_[:, 0:nsz], g, g[:, 0:nsz], ALU.mult)
                else:
                    t = self.bring.next()
                    self.tt(t, t[:, 0:nsz], pp_, pp_[:, 0:nsz], g, g[:, 0:nsz], ALU.mult)
                    self.tt(self.merged, ma, self.merged, ma, t, t[:, 0:nsz], ALU.add)

    def lru(self, l, b, tiles, mode):
        P, A = self.P, self.A
        A.reset()
        ll = 32 if l % 2 else 64
        ya = A.view("ya", [128, 8, T], BF16)
        xa = A.view("xa", [128, T], F32); xab = A.view("xab", [128, T], BF16)
        rt = A.view("rt", [128, T], F32); it = A.view("it", [128, T], F32); tmp = A.view("ltmp", [128, T], F32)
        hf = A.view("hf", [128, T], F32); hbk = A.view("hbk", [128, T], F32)
        for n in range(8):
            lw = self.lring.next()
            P.dma("pool", lw, lw[:, :, :], self.lruw_d, self.lruw_d[l, n])
            self.win(l, BIDX["lx"] + n, lambda ps, n0, nsz: self.conv(l, ps, n0, nsz, xa, xa[:, n0:n0 + nsz], "caw", "cab", n, ll))
            self.cp(xab, xab[:, :], xa, xa[:, :], eng="act")
            for d in range(2):
                for (n0, nsz) in NTILES:
                    for (ri, dst, bn) in ((0, rt, "lbr"), (1, it, "lbi")):
                        ps = self.pmm.next()
                        self.mm(ps, [(ps[:, 0:nsz], [(lw[:, ri * 2 + d, :], xab[:, n0:n0 + nsz])])], [lw, xab])
                        self.act(dst, dst[:, n0:n0 + nsz], ps, ps[:, 0:nsz], AF.Sigmoid, bias=self.ppc(l, bn, d * 8 + n),
                                 rd=[self.pp])
                cl = self.pd[:, l * 32 + d * 8 + n: l * 32 + d * 8 + n + 1]
                self.act(rt, rt[:, :], rt, rt[:, :], AF.Exp, scale=cl, rd=[self.pd])
                self.act(tmp, tmp[:, :], rt, rt[:, :], AF.Square)
                self.ts(tmp, tmp[:, :], tmp, tmp[:, :], -1.0, 1.0, ALU.mult, ALU.add)
                self.ts(tmp, tmp[:, :], tmp, tmp[:, :], 0.0, None, ALU.max)
                self.act(tmp, tmp[:, :], tmp, tmp[:, :], AF.Sqrt)
                self.tt(it, it[:, :], it, it[:, :], xa, xa[:, :], ALU.mult)
                self.tt(it, it[:, :], it, it[:, :], tmp, tmp[:, :], ALU.mult)
                if d == 0:
                    P.op("dve", lambda e: e.tensor_tensor_scan(hf[:, :], rt[:, :], it[:, :], 0.0, ALU.mult, ALU.add),
                         [rt, it], [hf])
                else:
                    P.op("dve", lambda e: e.tensor_tensor_scan(hbk[:, CT - 1::-1], rt[:, CT - 1::-1], it[:, CT - 1::-1],
                                                               0.0, ALU.mult, ALU.add), [rt, it], [hbk])
                    P.op("dve", lambda e: e.tensor_tensor_scan(hbk[:, T - 1:CT - 1:-1], rt[:, T - 1:CT - 1:-1],
                                                               it[:, T - 1:CT - 1:-1], hbk[:, 0:1], ALU.mult, ALU.add),
                         [rt, it, hbk], [hbk])
            self.tt(hf, hf[:, :], hf, hf[:, :], hbk, hbk[:, :], ALU.add)

            def cons(ps, n0, nsz, n=n):
                g = self.bring.next()
                self.act(g, g[:, 0:nsz], ps, ps[:, 0:nsz], AF.Silu)
                self.tt(ya, ya[:, n, n0:n0 + nsz], hf, hf[:, n0:n0 + nsz], g, g[:, 0:nsz], ALU.mult)
            self.win(l, BIDX["lg"] + n, cons)
        self.project(l, ya, ya, 8, self.wpa_d, 0, mode, tiles)

    def gla(self, l, b, tiles, mode):
        P, A = self.P, self.A
        A.reset()
        ob = A.view("ob", [128, 8, T], BF16)
        low = [A.view("low%d" % d, [16, T], BF16) for d in range(2)]
        vtok = A.view("vtok", [128, NCH, 256], BF16)
        qT = A.view("qT", [128, T], BF16); kT = A.view("kT", [128, T], BF16)
        qe = A.view("qe", [128, T], BF16); ke = A.view("ke", [128, T], BF16); kd = A.view("kd", [128, T], BF16)
        tA = A.view("tA", [128, T], F32); tB = A.view("tB", [128, T], F32); G = A.view("G", [128, T + 1], F32)
        vTa = tB.h.bitcast(BF16)[:, 0:T]
        ebt = A.view("ebt", [128, NCH], F32)
        Sf = A.view("Sf", [128, 256], F32); Sb = A.view("Sbf", [128, 256], BF16)
        kdt = Ring([A.view("kdt%d" % i, [128, 128], BF16) for i in range(2)])
        atm = Ring([A.view("atm%d" % i, [128, 128], BF16) for i in range(2)])
        P.dma("pool", self.gup, self.gup[:, :, :], self.gup_d, self.gup_d[l])
        for d in range(2):
            self.win(l, BIDX["low"] + d, lambda ps, n0, nsz, d=d: self.cp(low[d], low[d][:, n0:n0 + nsz], ps, ps[0:16, 0:nsz], eng="act"))
        for h in range(4):
            self.win(l, BIDX["q"] + h, lambda ps, n0, nsz: self.act(qT, qT[:, n0:n0 + nsz], ps, ps[:, 0:nsz], AF.Copy, scale=128.0 ** -0.5))
            self.win(l, BIDX["k"] + h, lambda ps, n0, nsz: self.cp(kT, kT[:, n0:n0 + nsz], ps, ps[:, 0:nsz], eng="act"))
            for j in range(2):
                self.win(l, BIDX["v"] + h * 2 + j, lambda ps, n0, nsz: self.cp(tB, vTa[:, n0:n0 + nsz], ps, ps[:, 0:nsz], eng="act"))
                for c0 in range(0, NCH, 8):
                    cn = min(8, NCH - c0)
                    self.tr(self.pT, [(self.pT[:, i * 128:(i + 1) * 128], vTa[:, (c0 + i) * 128:(c0 + i + 1) * 128]) for i in range(cn)],
                            [tB, self.ident], self.ident[:, :])
                    self.cp(vtok, vtok[:, c0:c0 + cn, j * 128:(j + 1) * 128],
                            self.pT, self.pT[:, 0:cn * 128].rearrange("p (a b) -> p a b", b=128))
            obh = ob[:, 2 * h:2 * h + 2, :]
            for d in range(2):
                nb = self.pd[:, l * 32 + 16 + d * 4 + h: l * 32 + 16 + d * 4 + h + 1]
                for (n0, nsz) in NTILES:
                    ps = self.pmm.next()
                    self.mm(ps, [(ps[:, 0:nsz], [(self.gup[:, d, h * 128:(h + 1) * 128], low[d][:, n0:n0 + nsz])])], [self.gup, low[d]])
                    self.act(tA, tA[:, n0:n0 + nsz], ps, ps[:, 0:nsz], AF.Exp, bias=nb, scale=-1.0, rd=[self.pd])
                self.act(tA, tA[:, :], tA, tA[:, :], AF.Ln, bias=1.0)
                onesb = self.onef[:, 0:1].to_broadcast([128, T])
                if d == 0:
                    P.op("dve", lambda e: e.memset(G[:, 0:1], 0.0), [], [G])
                    P.op("dve", lambda e: e.tensor_tensor_scan(G[:, 1:T + 1], onesb, tA[:, :], 0.0, ALU.mult, ALU.add), [tA, self.onef], [G])
                    hi = G[:, 1:T + 1].rearrange("p (c j) -> p c j", j=128)
                    lo = G[:, 0:T:128].unsqueeze(2).to_broadcast([128, NCH, 128])
                    tot = tA[:, 127:T:128]
                else:
                    P.op("dve", lambda e: e.memset(G[:, T:T + 1], 0.0), [], [G])
                    P.op("dve", lambda e: e.tensor_tensor_scan(G[:, T - 1::-1], onesb, tA[:, ::-1], 0.0, ALU.mult, ALU.add), [tA, self.onef], [G])
                    hi = G[:, 0:T].rearrange("p (c j) -> p c j", j=128)
                    lo = G[:, 128:T + 1:128].unsqueeze(2).to_broadcast([128, NCH, 128])
                    tot = tA[:, 0:T:128]
                tA3 = tA[:, :].rearrange("p (c j) -> p c j", j=128)
                tB3 = tB[:, :].rearrange("p (c j) -> p c j", j=128)
                self.tt(tA, tA3, G, hi, G, lo, ALU.subtract)
                self.act(tB, tB[:, :], tA, tA[:, :], AF.Exp, scale=-1.0 / 16)
                self.tt(qe, qe[:, :], qT, qT[:, :], tB, tB[:, :], ALU.mult)
                self.act(tB, tB[:, :], tA, tA[:, :], AF.Exp, scale=1.0 / 16)
                self.tt(ke, ke[:, :], kT, kT[:, :], tB, tB[:, :], ALU.mult)
                self.tt(tB, tB3, tA, tA3, tA, tot.unsqueeze(2).to_broadcast([128, NCH, 128]), ALU.subtract)
                self.act(tB, tB[:, :], tB, tB[:, :], AF.Exp, scale=1.0 / 16)
                self.tt(kd, kd[:, :], kT, kT[:, :], tB, tB[:, :], ALU.mult)
                self.act(ebt, ebt[:, :], tA, tot, AF.Exp, scale=-1.0 / 16)
                P.op("dve", lambda e: e.memset(Sf[:, :], 0.0), [], [Sf])
                P.op("dve", lambda e: e.memset(Sb[:, :], 0.0), [], [Sb])
                order = list(range(NCH)) if d == 0 else [1, 0] + list(range(NCH - 1, 1, -1))
                msk = self.UF if d == 0 else self.UB
                pSb = [self.pmm.t[0], self.pmm.t[1]]
                pa = self.pmm.t[2]

                def gA(idx):
                    c = order[idx]; k = idx % 2
                    cs = slice(c * 128, (c + 1) * 128)
                    kt = kdt.t[k]; am = atm.t[k]; pS = pSb[k]
                    self.tr(self.pT, [(self.pT[:, 0:128], kd[:, cs])], [kd, self.ident], self.ident[:, :])
                    self.mm(pa, [(pa[:, 0:128], [(ke[:, cs], qe[:, cs])])], [ke, qe])
                    self.cp(kt, kt[:, :], self.pT, self.pT[:, 0:128], eng="act")
                    self.tt(am, am[:, :], pa, pa[:, 0:128], self.mk, msk, ALU.mult)
                    self.mm(pS, [(pS[:, 0:256], [(kt[:, :], vtok[:, c, :])])], [kt, vtok])

                def gB(idx):
                    c = order[idx]; k = idx % 2
                    cs = slice(c * 128, (c + 1) * 128)
                    am = atm.t[k]; pS = pSb[k]
                    po = self.pA
                    self.mm(po, [(po[:, j * 128:(j + 1) * 128], [(vtok[:, c, j * 128:(j + 1) * 128], am[:, :]),
                                                                  (Sb[:, j * 128:(j + 1) * 128], qe[:, cs])]) for j in range(2)],
                            [vtok, am, Sb, qe])
                    self.stt(Sf, Sf[:, :], Sf, Sf[:, :], ebt[:, c:c + 1], pS, pS[:, 0:256], ALU.mult, ALU.add, rd=[ebt])
                    self.cp(Sb, Sb[:, :], Sf, Sf[:, :], eng="act")
                    po3 = po[:, 0:256].rearrange("p (a b) -> p a b", b=128)
                    if d == 0:
                        self.cp(ob, obh[:, :, cs], po, po3, eng="act")
                    else:
                        self.tt(ob, obh[:, :, cs], po, po3, ob, obh[:, :, cs], ALU.add)

                gA(0)
                for idx in range(NCH):
                    if idx + 1 < NCH:
                        gA(idx + 1)
                    gB(idx)
            for (n0, nsz) in NTILES:
                pss = self.pmm.next()
                sqs = []
                for j in range(2):
                    sq = self.bring.next()
                    self.tt(sq, sq[:, 0:nsz], ob, obh[:, j, n0:n0 + nsz], ob, obh[:, j, n0:n0 + nsz], ALU.mult)
                    sqs.append(sq)
                self.mm(pss, [(pss[:, 0:nsz], [(self.ones[:, :], sq[:, 0:nsz]) for sq in sqs])], sqs + [self.ones])
                self.ts(tA, tA[:, n0:n0 + nsz], pss, pss[:, 0:nsz], 1.0 / 256, EPS, ALU.mult, ALU.add)
            self.act(tA, tA[:, :], tA, tA[:, :], AF.Sqrt)
            P.op("dve", lambda e: e.reciprocal(tA[:, :], tA[:, :]), [tA], [tA])
            for j in range(2):
                def cons(ps, n0, nsz, j=j):
                    g = self.bring.next()
                    self.act(g, g[:, 0:nsz], ps, ps[:, 0:nsz], AF.Silu)
                    oa = obh[:, j, n0:n0 + nsz]
                    self.stt(ob, oa, ob, oa, self.ppc(l, "gng", j), tA, tA[:, n0:n0 + nsz], ALU.mult, ALU.mult, rd=[self.pp])
                    self.tt(ob, oa, ob, oa, g, g[:, 0:nsz], ALU.mult)
                self.win(l, BIDX["gg"] + h * 2 + j, cons)
        self.project(l, ob, ob, 8, self.wpb_d, 1, mode, tiles)

    def ssd(self, l, b, tiles):
        P, A = self.P, self.A
        A.reset()
        ll = 32 if l % 2 else 64
        bo = A.view("bo", [128, 4, T], BF16)
        xtok = A.view("xtok", [128, NCH, 512], BF16)
        BT = A.view("BT", [128, T], BF16); CTt = A.view("CT", [128, T], BF16); Btok = A.view("Btok", [128, NCH, 128], BF16)
        dtk = A.view("dtk", [128, NCH, 64], F32); adk = A.view("adk", [128, NCH, 64], F32)
        adb = A.view("adb", [128, NCH, 64], BF16); ddt = A.view("ddt", [128, NCH, 64], F32)
        dtot = A.view("dtot", [128, NCH, 64], F32)
        xsf = Ring([A.view("xsf%d" % i, [128, T], BF16) for i in range(1)])
        ctmp = Ring([A.view("ctmp%d" % i, [128, 512], F32) for i in range(2)])
        import os
        PE2 = os.environ.get("PE2", "pool")
        Rs = [A.view("R%d" % i, [128, 8, 128], BF16) for i in range(2)]
        LTs = [A.view("LT%d" % i, [128, 8, 128], BF16) for i in range(2)]
        ECs = [A.view("EC%d" % i, [128, 8, 128], BF16) for i in range(2)]
        cbs = [A.view("cbm%d" % i, [128, 128], BF16) for i in range(2)]
        xdts = [A.view("xdt%d" % i, [128, 8, 64], BF16) for i in range(2)]
        xdds = [A.view("xdd%d" % i, [128, 8, 64], BF16) for i in range(2)]
        xdr = Ring(xdts + xdds)
        Sfs = [A.view("Sf%d" % i, [128, 8, 64], F32) for i in range(2)]
        Sbs = [A.view("Sbf%d" % i, [128, 512], BF16) for i in range(2)]
        rs = A.view("rs", [128, 512], F32)
        sdb = A.view("sdbc", [128, 128], F32)
        P.dma("sp", sdb, sdb[:, :], self.rows_d, self.rows_d[l, 1:2, 1024:1152].partition_broadcast(128))
        self.act(sdb, sdb[:, 64:128], sdb, sdb[:, 64:128], AF.Exp)
        self.ts(sdb, sdb[:, 64:128], sdb, sdb[:, 64:128], -1.0, None, ALU.mult)
        wdt = self.wring.next()
        P.dma("pool", wdt, wdt[:, :, :], self.win_d, self.win_d[l, BIDX["dt"]])
        for c0 in range(0, NCH, 8):
            cn = min(8, NCH - c0)
            ps = self.pmm.next()
            self.mm(ps, [(ps[:, i * 64:(i + 1) * 64], [(self.hT[:, k, (c0 + i) * 128:(c0 + i + 1) * 128], wdt[:, k, 0:64]) for k in range(8)])
                         for i in range(cn)], [self.hT, wdt])
            da = dtk[:, c0:c0 + cn, :]
            self.tt(dtk, da, ps, ps[:, 0:cn * 64].rearrange("p (a b) -> p a b", b=64), sdb,
                    sdb[:, 0:64].unsqueeze(1).to_broadcast([128, cn, 64]), ALU.add)
            self.act(dtk, da, dtk, da, AF.Exp)
            self.act(dtk, da, dtk, da, AF.Ln, bias=1.0)
        self.ck("dt0")
        self.tt(adk, adk[:, :, :], dtk, dtk[:, :, :], sdb, sdb[:, 64:128].unsqueeze(1).to_broadcast([128, NCH, 64]), ALU.mult)
        self.ck("dt1")
        self.cp(adb, adb[:, :, :], adk, adk[:, :, :])
        for c0 in range(0, NCH, 8):
            cn = min(8, NCH - c0)
            ps = self.pmm.next()
            self.mm(ps, [(ps[:, i * 64 + d * 32: i * 64 + d * 32 + 32], [((self.TL if d == 0 else self.TU), adb[:, c0 + i, d * 32:(d + 1) * 32])])
                         for i in range(cn) for d in range(2)], [self.mk, adb])
            self.act(ddt, ddt[:, c0:c0 + cn, :], ps, ps[:, 0:cn * 64].rearrange("p (a b) -> p a b", b=64), AF.Exp)
            ps2 = self.pmm.next()
            self.mm(ps2, [(ps2[:, i * 64:(i + 1) * 64], [(self.ones[:, :], adb[:, c0 + i, :])]) for i in range(cn)], [self.ones, adb])
            self.act(dtot, dtot[:, c0:c0 + cn, :], ps2, ps2[:, 0:cn * 64].rearrange("p (a b) -> p a b", b=64), AF.Exp)
        self.tt(ddt, ddt[:, :, :], ddt, ddt[:, :, :], dtk, dtk[:, :, :], ALU.mult)
        self.ck("dt2")

        for g in range(4):
            for (dst, bi) in ((BT, 16 + g), (CTt, 20 + g)):
                def cbc(ps, n0, nsz, dst=dst, bi=bi):
                    ct = ctmp.next()
                    self.conv(l, ps, n0, nsz, ct, ct[:, 0:nsz], "ccw", "ccb", bi, ll)
                    self.act(dst, dst[:, n0:n0 + nsz], ct, ct[:, 0:nsz], AF.Silu)
                self.win(l, BIDX["xbc"] + bi, cbc)
            for c0 in range(0, NCH, 8):
                cn = min(8, NCH - c0)
                self.tr(self.pT, [(self.pT[:, i * 128:(i + 1) * 128], BT[:, (c0 + i) * 128:(c0 + i + 1) * 128]) for i in range(cn)],
                        [BT, self.ident], self.ident[:, :])
                self.cp(Btok, Btok[:, c0:c0 + cn, :], self.pT, self.pT[:, 0:cn * 128].rearrange("p (a b) -> p a b", b=128), eng="act")
            self.ck("bc")
            for i in range(4):
                bi = g * 4 + i
                xs = xsf.next()

                def cxs(ps, n0, nsz, xs=xs, bi=bi):
                    ct = ctmp.next()
                    self.conv(l, ps, n0, nsz, ct, ct[:, 0:nsz], "ccw", "ccb", bi, ll)
                    self.act(xs, xs[:, n0:n0 + nsz], ct, ct[:, 0:nsz], AF.Silu)
                self.win(l, BIDX["xbc"] + bi, cxs)
                self.ts(bo, bo[:, i, :], xs, xs[:, :], self.ppc(l, "sdd", bi), None, ALU.mult, rd=[self.pp])
                for c0 in range(0, NCH, 8):
                    cn = min(8, NCH - c0)
                    self.tr(self.pT, [(self.pT[:, k * 128:(k + 1) * 128], xs[:, (c0 + k) * 128:(c0 + k + 1) * 128]) for k in range(cn)],
                            [xs, self.ident], self.ident[:, :])
                    self.cp(xtok, xtok[:, c0:c0 + cn, i * 128:(i + 1) * 128], self.pT,
                            self.pT[:, 0:cn * 128].rearrange("p (a b) -> p a b", b=128), eng=("act" if (c0 // 8) % 2 else "dve"))
            self.ck("xs")
            orders = [list(range(NCH)), [1, 0] + list(range(NCH - 1, 1, -1))]
            pbufs = [(t_, t_.h[:, :]) for t_ in self.pmm.t] + [(self.pT, self.pT.h[:, :].bitcast(F32))]
            for d in range(2):
                Sf = Sfs[0]; Sb = Sbs[0]
                P.op("dve", lambda e: e.memset(Sfs[0][:, :, :], 0.0), [], [Sf])
                P.op("dve", lambda e: e.memset(Sbs[0][:, :], 0.0), [], [Sb])
                U = self.UF if d == 0 else self.UB
                W = self.TL if d == 0 else self.TU
                hc = slice(d * 32 + g * 8, d * 32 + g * 8 + 8)

                def stageA(idx):
                    c = orders[d][idx]; k = idx % 2
                    cs = slice(c * 128, (c + 1) * 128)
                    R = Rs[k]; LT = LTs[k]; EC = ECs[k]; cbm = cbs[k]; xdt = xdts[k]; xdd = xdds[k]
                    pcy_t, pcy = pbufs[2 + k]
                    pst_t, pst = pbufs[k]
                    x3 = xtok[:, c, :].rearrange("p (a b) -> p a b", b=64)
                    self.tt(xdd, xdd[:, :, :], xtok, x3, ddt, ddt[:, c, hc].unsqueeze(2).to_broadcast([128, 8, 64]), ALU.mult, eng=PE2)
                    self.tt(xdt, xdt[:, :, :], xtok, x3, dtk, dtk[:, c, hc].unsqueeze(2).to_broadcast([128, 8, 64]), ALU.mult, eng=PE2)
                    self.tt(R, R[:, :, :], adk, adk[:, c, hc].unsqueeze(2).to_broadcast([128, 8, 128]),
                            self.mk, U.unsqueeze(1).to_broadcast([128, 8, 128]), ALU.mult)
                    self.mm(pcy_t, [(pcy[:, 0:128], [(BT[:, cs], CTt[:, cs])])], [BT, CTt])
                    Rf = R[:, :, :].rearrange("p a b -> p (a b)")
                    self.mm(self.pA, [(self.pA[:, hh * 512:(hh + 1) * 512], [(W, Rf[:, hh * 512:(hh + 1) * 512])]) for hh in range(2)],
                            [self.mk, R])
                    self.mm(self.pB, [(self.pB[:, hh * 512:(hh + 1) * 512], [(self.ones[:, :], Rf[:, hh * 512:(hh + 1) * 512])]) for hh in range(2)],
                            [self.ones, R])
                    self.mm(pst_t, [(pst[:, :], [(Btok[:, c, :], xdd[:, :, :].rearrange("p a b -> p (a b)"))])], [Btok, xdd])
                    self.tt(cbm, cbm[:, :], pcy_t, pcy[:, 0:128], self.mk, U, ALU.mult)
                    self.act(LT, LT[:, :, :], self.pA, self.pA[:, :].rearrange("p (a b) -> p a b", b=128), AF.Exp)
                    self.act(EC, EC[:, :, :], self.pB, self.pB[:, :].rearrange("p (a b) -> p a b", b=128), AF.Exp)
                    self.tt(LT, LT[:, :, :], LT, LT[:, :, :], cbm, cbm[:, :].unsqueeze(1).to_broadcast([128, 8, 128]), ALU.mult)
                    self.tt(EC, EC[:, :, :], EC, EC[:, :, :], CTt, CTt[:, cs].unsqueeze(1).to_broadcast([128, 8, 128]), ALU.mult, eng=PE2)

                def stageB(idx):
                    c = orders[d][idx]; k = idx % 2
                    cs = slice(c * 128, (c + 1) * 128)
                    LT = LTs[k]; EC = ECs[k]; xdt = xdts[k]
                    pcy_t, pcy = pbufs[2 + k]
                    pst_t, pst = pbufs[k]
                    groups = []
                    for hp in range(4):
                        for e_ in range(2):
                            hh = 2 * hp + e_
                            groups.append((pcy[e_ * 64:(e_ + 1) * 64, hp * 128:(hp + 1) * 128],
                                           [(xdt[:, hh, :], LT[:, hh, :]), (Sb[:, hh * 64:(hh + 1) * 64], EC[:, hh, :])]))
                    self.mm(pcy_t, groups, [xdt, LT, Sb, EC])
                    self.tt(Sf, Sf[:, :, :], Sf, Sf[:, :, :], dtot, dtot[:, c, hc].unsqueeze(2).to_broadcast([128, 8, 64]), ALU.mult)
                    self.tt(Sf, Sf[:, :, :], Sf, Sf[:, :, :], pst_t, pst[:, :].rearrange("p (a b) -> p a b", b=64), ALU.add)
                    self.cp(Sb, Sb[:, :], Sf, Sf[:, :, :].rearrange("p a b -> p (a b)"), eng="act")
                    self.tt(bo, bo[:, :, cs], pcy_t, pcy[:, :].rearrange("p (a b) -> p a b", b=128), bo, bo[:, :, cs], ALU.add)

                stageA(0)
                for idx in range(NCH):
                    if idx + 1 < NCH:
                        stageA(idx + 1)
                    stageB(idx)
            self.ck("scan")
            for i in range(4):
                def cz(ps, n0, nsz, i=i):
                    gt = self.bring.next()
                    self.act(gt, gt[:, 0:nsz], ps, ps[:, 0:nsz], AF.Silu)
                    self.tt(bo, bo[:, i, n0:n0 + nsz], bo, bo[:, i, n0:n0 + nsz], gt, gt[:, 0:nsz], ALU.mult)
                self.win(l, BIDX["z"] + g * 4 + i, cz)
            for (n0, nsz) in NTILES:
                pss = self.pmm.next()
                sqs = []
                for i in range(4):
                    sq = xdr.next()
                    sqa = sq[:, :, :].rearrange("p a b -> p (a b)")[:, 0:nsz]
                    self.tt(sq, sqa, bo, bo[:, i, n0:n0 + nsz], bo, bo[:, i, n0:n0 + nsz], ALU.mult)
                    sqs.append((sq, sqa))
                self.mm(pss, [(pss[:, 0:nsz], [(self.ones[:, :], a_) for (_, a_) in sqs])], [s_ for (s_, _) in sqs] + [self.ones])
                self.ts(rs, rs[:, 0:nsz], pss, pss[:, 0:nsz], 1.0 / 512, EPS, ALU.mult, ALU.add)
                self.act(rs, rs[:, 0:nsz], rs, rs[:, 0:nsz], AF.Sqrt)
                P.op("dve", lambda e, nsz=nsz: e.reciprocal(rs[:, 0:nsz], rs[:, 0:nsz]), [rs], [rs])
                for i in range(4):
                    oa = bo[:, i, n0:n0 + nsz]
                    self.stt(bo, oa, bo, oa, self.ppc(l, "sng", g * 4 + i), rs, rs[:, 0:nsz], ALU.mult, ALU.mult, rd=[self.pp])
            self.ck("norm")
            for m in range(8):
                wt = self.pring.next()
                P.dma("pool", wt, wt[:, 0:4, :], self.wpc_d, self.wpc_d[l, m, :, g * 4:(g + 1) * 4, :])
                for (n0, nsz) in tiles:
                    pp_ = self.pmm.next()
                    self.mm(pp_, [(pp_[:, 0:nsz], [(wt[:, k, :], bo[:, k, n0:n0 + nsz]) for k in range(4)])], [wt, bo])
                    ma = self.merged[:, m, n0:n0 + nsz]
                    if g == 0:
                        self.cp(self.merged, ma, pp_, pp_[:, 0:nsz], eng="act")
                    else:
                        self.tt(self.merged, ma, pp_, pp_[:, 0:nsz], self.merged, ma, ALU.add)
        for m in range(8):
            def cg(ps, n0, nsz, m=m):
                gt = self.bring.next()
                self.act(gt, gt[:, 0:nsz], ps, ps[:, 0:nsz], AF.Sigmoid)
                ma = self.merged[:, m, n0:n0 + nsz]
                self.tt(self.merged, ma, self.merged, ma, gt, gt[:, 0:nsz], ALU.mult)
            self.win(l, BIDX["mg"] + 16 + m, cg, tiles)

    def final(self, l, b, skip_ctx):
        P, A = self.P, self.A
        A.reset()
        wo = A.view("wo", [128, 8, D], BF16)
        P.dma("pool", wo, wo[:, :, :], self.wout_d, self.wout_d[l])
        Gb = A.view("Gb", [128, D], F32); Gc = A.view("Gc", [128, D], F32)
        self.bload(Gb, l, b, 2); self.bload(Gc, l, 2, 2)
        xr = Ring([A.view("xf%d" % i, [128, D], F32) for i in range(3)])
        tr_ = Ring([A.view("tf%d" % i, [128, D], F32) for i in range(2)])
        junk = A.view("junkf", [128, D], BF16)
        for j in range(2 if skip_ctx else 0, NCH):
            xt = xr.next(); t = tr_.next(); st = self.stat.next()
            self.xdma(l, b, j, xt, xt[:, :], True, None)
            cs = slice(j * 128, (j + 1) * 128)
            self.mm(self.pA, [(self.pA[:, hh * 512:(hh + 1) * 512], [(self.merged[:, k, cs], wo[:, k, hh * 512:(hh + 1) * 512]) for k in range(8)])
                              for hh in range(2)], [self.merged, wo])
            P.op("dve", lambda e, st=st: e.memset(st[:, 0:1], 0.0), [], [st])
            self.act(junk, junk[:, :], self.pA, self.pA[:, :], AF.Square, accum=st[:, 0:1], wr=[st])
            self.rstd(st, 0, 1.0 / D)
            Ga = Gc if j < 2 else Gb
            self.stt(t, t[:, :], self.pA, self.pA[:, :], st[:, 0:1], Ga, Ga[:, :], ALU.mult, ALU.mult, rd=[st])
            self.tt(t, t[:, :], t, t[:, :], xt, xt[:, :], ALU.add)
            self.xdma(l, b, j, t, t[:, :], False, None)

    def build(self, stop=99, nb=2):
        try:
            self.build_(stop, nb)
        except StopBuild:
            pass
        P = self.P
        self.A.reset()
        for eng in ("sp",):
            w = P._waits(eng, [self.out_d], [self.out_d])
            P.ops[eng].append((w, None, None, 0))
        P.emit()
        return self.nc

    def build_(self, stop=99, nb=2):
        P = self.P
        self.setup()
        if stop >= 1:
            self.adaln()
        for l in range(self.nl):
            last = (l == self.nl - 1)
            tiles = NTILES[1:] if last else NTILES
            for b in range(nb):
                if stop >= 2:
                    self.phaseA(l, b)
                if stop >= 3:
                    self.ssd(l, b, tiles)
                if stop >= 4:
                    self.gla(l, b, tiles, "add")
                if stop >= 5:
                    self.lru(l, b, tiles, "add")
                if stop >= 6:
                    self.final(l, b, last)


def prep_shared(inp):
    f = lambda a: np.ascontiguousarray(np.asarray(a, dtype=np.float32))
    w_in = f(inp["w_in"])
    win = np.zeros((L, NBLK, 128, 8, 128), np.float32)
    for i, (c0, m) in enumerate(BLKS):
        win[:, i, :, :, :m] = w_in[:, :, c0:c0 + m].reshape(L, 8, 128, m).transpose(0, 2, 1, 3)
    sh = {"w_in": win}
    sh["ada_w"] = f(f(inp["ada_w"]).reshape(L, 8, 128, 3072).transpose(0, 2, 1, 3))
    rows = np.zeros((L, 3, 3072), np.float32)
    rows[:, 0, :] = f(inp["ada_b"]); rows[:, 1, :1024] = f(inp["pre_g"]); rows[:, 2, :1024] = f(inp["post_g"])
    rows[:, 1, 1024:1088] = f(inp["ssd_dt_bias"]).reshape(L, 64); rows[:, 1, 1088:1152] = f(inp["ssd_a_log"]).reshape(L, 64)
    sh["rows"] = rows
    pp = np.zeros((128, L, PPW), np.float32)

    def put(name, arr):
        n = arr.shape[1]
        pp[:, :, PPL[name]:PPL[name] + n] = arr.transpose(2, 0, 1)
    put("caw", f(inp["conv_a_w"]).reshape(L, 4, 8, 128).transpose(0, 2, 1, 3).reshape(L, 32, 128))
    put("cab", f(inp["conv_a_b"]).reshape(L, 8, 128))
    put("lbr", f(inp["lru_br"]).reshape(L, 16, 128))
    put("lbi", f(inp["lru_bi"]).reshape(L, 16, 128))
    put("llam", f(inp["lru_lam"]).reshape(L, 16, 128))
    put("gab", f(inp["gla_alpha_b"]).reshape(L, 8, 128))
    put("gng", f(inp["gla_norm_g"]).reshape(L, 2, 128))
    put("ccw", f(inp["conv_c_w"]).reshape(L, 4, 24, 128).transpose(0, 2, 1, 3).reshape(L, 96, 128))
    put("ccb", f(inp["conv_c_b"]).reshape(L, 24, 128))
    put("sdd", np.repeat(f(inp["ssd_d"]).reshape(L, 16, 2), 64, axis=2))
    put("sng", f(inp["ssd_norm_g"]).reshape(L, 16, 128))
    z64 = np.zeros((L, 1, 64), np.float32)
    put("sal", np.concatenate([f(inp["ssd_a_log"]).reshape(L, 1, 64), z64], 2))
    put("sdb", np.concatenate([f(inp["ssd_dt_bias"]).reshape(L, 1, 64), z64], 2))
    sh["pp"] = np.ascontiguousarray(pp.reshape(128, NPP))
    wr = f(inp["lru_wr"]); wi = f(inp["lru_wi"])
    lw = np.stack([wr, wi], 1)
    sh["lruw"] = np.ascontiguousarray(lw.transpose(0, 3, 4, 1, 2, 5).reshape(L, 8, 128, 4, 128))
    sh["gup"] = np.ascontiguousarray(f(inp["gla_alpha_up"]).transpose(0, 2, 1, 3))
    for nm, key, nk in (("wpa", "w_pa", 8), ("wpb", "w_pb", 8), ("wpc", "w_pc", 16)):
        w = f(inp[key]).reshape(L, nk, 128, 8, 128)
        sh[nm] = np.ascontiguousarray(w.transpose(0, 3, 2, 1, 4))
    sh["wout"] = np.ascontiguousarray(f(inp["w_out"]).reshape(L, 8, 128, 1024).transpose(0, 2, 1, 3))
    return sh


def core_inputs(inp, sh, i):
    f = lambda a: np.ascontiguousarray(np.asarray(a, dtype=np.float32))
    m = dict(sh)
    m["x"] = f(inp["x"][2 * i:2 * i + 2])
    m["ctx"] = f(inp["ctx"][2 * i:2 * i + 2])
    crow = np.stack([f(inp["c"][2 * i]), f(inp["c"][2 * i + 1]), f(inp["c_ctx"])], 0)
    m["cT"] = np.ascontiguousarray(crow.reshape(3, 8, 128).transpose(2, 1, 0))
    return m


_NC = {}


def kernel(**inputs):
    if "nc" not in _NC:
        _NC["nc"] = K(L).build()
    sh = prep_shared(inputs)
    in_maps = [core_inputs(inputs, sh, i) for i in range(8)]
    res = run_bass_kernel_spmd(_NC["nc"], in_maps, core_ids=list(range(8)))
    return np.concatenate([r["out"] for r in res.results], axis=0).astype(np.float32)
```
